# Optimizing a Trainium2 kernel written in Bass

```python
import jax, jax.numpy as jnp
from jax import lax

D_MODEL = 1024
BATCH = 2
SEQ = 16384
DEPTH = 2
DEC_BATCH = 8
DEC_SEQ = 64
PAST_LEN = 1024

CHUNK = 64
BRANCH_W = D_MODEL // 2
N_BRANCH = 3
GMLP_CHUNK = 128
GMLP_GROUPS = 4
GMLP_GW = BRANCH_W // GMLP_GROUPS
CONV_W = 31
N_HEADS = 8
HEAD_DIM = BRANCH_W // N_HEADS
BAND_CHUNKS = 8
WINDOW = BAND_CHUNKS * CHUNK
BAND = WINDOW + CHUNK
MAX_REL = 128
FFN_HIDDEN = ((8 * D_MODEL // 3 + 255) // 256) * 256
IN_COLS = 7 * BRANCH_W + N_BRANCH * D_MODEL
ATTN_SCALE = HEAD_DIM ** -0.5
NEG_INF = -1e30
EPS = 1e-6

kernel_name = "chunk_causal_hybrid_gated_encoder_step"


def rms_norm(x, g):
    xf = x.astype(jnp.float32)
    y = xf * lax.rsqrt(jnp.mean(xf * xf, axis=-1, keepdims=True) + EPS)
    return (y * g.astype(jnp.float32)).astype(x.dtype)


def gmlp_spatial(u, v_n, ws, bs):
    b, l, w = u.shape
    c = min(l, GMLP_CHUNK)
    n = l // c
    vr = v_n.reshape(b, n, c, GMLP_GROUPS, GMLP_GW)
    wm = jnp.tril(ws[:, :c, :c])
    mixed = jnp.einsum('gij,bnjgc->bnigc', wm, vr) + jnp.transpose(bs[:, :c])[None, None, :, :, None]
    return u * mixed.reshape(b, l, w)


def depthwise_causal_conv(x_padded, w, bias):
    y = lax.conv_general_dilated(x_padded, w[:, None, :], window_strides=(1,), padding='VALID',
                                 dimension_numbers=('NWC', 'WIO', 'NWC'), feature_group_count=BRANCH_W)
    return y + bias


def rel_bias_lookup(rel_bias, dist):
    return rel_bias[:, jnp.clip(dist, -MAX_REL, MAX_REL) + MAX_REL]


def band_attention_prompt(q, k, v, rel_bias):
    b, s, h, dh = q.shape
    nc = s // CHUNK
    pad = ((0, 0), (WINDOW, 0), (0, 0), (0, 0))
    kp = jnp.pad(k, pad)
    vp = jnp.pad(v, pad)
    idx = jnp.arange(nc)[:, None] * CHUNK + jnp.arange(BAND)[None, :]
    kb = kp[:, idx]
    vb = vp[:, idx]
    qc = q.reshape(b, nc, CHUNK, h, dh)
    scores = jnp.einsum('bcqhd,bckhd->bhcqk', qc, kb, preferred_element_type=jnp.float32) * ATTN_SCALE
    dist = WINDOW + jnp.arange(CHUNK)[:, None] - jnp.arange(BAND)[None, :]
    scores = scores + rel_bias_lookup(rel_bias, dist).astype(jnp.float32)[None, :, None]
    valid = (idx >= WINDOW)[None, None, :, None, :]
    scores = jnp.where(valid, scores, NEG_INF)
    p = jax.nn.softmax(scores, axis=-1).astype(v.dtype)
    o = jnp.einsum('bhcqk,bckhd->bcqhd', p, vb)
    return o.reshape(b, s, h * dh)


def attention_sample(q, k_all, v_all, rel_bias):
    b, t, h, dh = q.shape
    r = k_all.shape[1] - t
    scores = jnp.einsum('bqhd,bkhd->bhqk', q, k_all, preferred_element_type=jnp.float32) * ATTN_SCALE
    dist = r + jnp.arange(t)[:, None] - jnp.arange(r + t)[None, :]
    scores = scores + rel_bias_lookup(rel_bias, dist).astype(jnp.float32)[None]
    p = jax.nn.softmax(scores, axis=-1).astype(v_all.dtype)
    o = jnp.einsum('bhqk,bkhd->bqhd', p, v_all)
    return o.reshape(b, t, h * dh)


def trunk_layer(x, cache_k, cache_v, cache_conv, norm_mix_g, w_in, b_gate, gmlp_norm_g, gmlp_ws, gmlp_bs,
                conv_dw, conv_b, conv_norm_g, q_norm_g, k_norm_g, rel_bias, w_branch, w_out,
                norm_ffn_g, w_gate_up, w_down):
    b, l, _ = x.shape
    W = BRANCH_W
    h = rms_norm(x, norm_mix_g)
    z = jnp.einsum('bld,dc->blc', h, w_in)
    u = jax.nn.gelu(z[..., 0:W])
    v = jax.nn.gelu(z[..., W:2 * W])
    glu = z[..., 2 * W:3 * W] * jax.nn.sigmoid(z[..., 3 * W:4 * W])
    q = z[..., 4 * W:5 * W].reshape(b, l, N_HEADS, HEAD_DIM)
    k = z[..., 5 * W:6 * W].reshape(b, l, N_HEADS, HEAD_DIM)
    va = z[..., 6 * W:7 * W].reshape(b, l, N_HEADS, HEAD_DIM)
    gates = jax.nn.sigmoid(z[..., 7 * W:] + b_gate).reshape(b, l, N_BRANCH, D_MODEL)

    v_n = rms_norm(v, gmlp_norm_g)
    y_a = gmlp_spatial(u, v_n, gmlp_ws, gmlp_bs)

    if cache_conv is None:
        conv_in = jnp.pad(glu, ((0, 0), (CONV_W - 1, 0), (0, 0)))
    else:
        conv_in = jnp.concatenate([cache_conv, glu], axis=1)
    new_conv = conv_in[:, -(CONV_W - 1):]
    y_b = jax.nn.silu(rms_norm(depthwise_causal_conv(conv_in, conv_dw, conv_b), conv_norm_g))

    q = rms_norm(q, q_norm_g)
    k = rms_norm(k, k_norm_g)
    if cache_k is None:
        y_c = band_attention_prompt(q, k, va, rel_bias)
        keep = min(WINDOW, l)
        new_k = k[:, l - keep:]
        new_v = va[:, l - keep:]
    else:
        k_all = jnp.concatenate([cache_k, k], axis=1)
        v_all = jnp.concatenate([cache_v, va], axis=1)
        y_c = attention_sample(q, k_all, v_all, rel_bias)
        new_k = k
        new_v = va

    merged = (gates[:, :, 0] * jnp.einsum('blw,wd->bld', y_a, w_branch[0])
              + gates[:, :, 1] * jnp.einsum('blw,wd->bld', y_b, w_branch[1])
              + gates[:, :, 2] * jnp.einsum('blw,wd->bld', y_c, w_branch[2]))
    x = x + jnp.einsum('bld,de->ble', merged, w_out)

    h2 = rms_norm(x, norm_ffn_g)
    gu = jnp.einsum('bld,df->blf', h2, w_gate_up)
    x = x + jnp.einsum('blf,fd->bld', jax.nn.silu(gu[..., :FFN_HIDDEN]) * gu[..., FFN_HIDDEN:], w_down)
    return x, new_k, new_v, new_conv, v_n


def setup_inputs(seed: int = 0) -> dict:
    key = jax.random.key(seed)
    ks = jax.random.split(key, 22)
    f32 = jnp.float32
    r = min(WINDOW, PAST_LEN)

    def nrm(k, shape, scale):
        return jax.random.normal(k, shape, f32) * scale

    return {
        "x_prompt": nrm(ks[0], (BATCH, SEQ, D_MODEL), 1.0),
        "x_sample": nrm(ks[1], (DEC_BATCH, DEC_SEQ, D_MODEL), 1.0),
        "cache_attn_k": nrm(ks[2], (DEPTH, DEC_BATCH, r, N_HEADS, HEAD_DIM), 1.0),
        "cache_attn_v": nrm(ks[3], (DEPTH, DEC_BATCH, r, N_HEADS, HEAD_DIM), 1.0),
        "cache_conv": nrm(ks[4], (DEPTH, DEC_BATCH, CONV_W - 1, BRANCH_W), 0.5),
        "norm_mix_g": 1.0 + nrm(ks[5], (DEPTH, D_MODEL), 0.02),
        "w_in": nrm(ks[6], (DEPTH, D_MODEL, IN_COLS), D_MODEL ** -0.5),
        "b_gate": nrm(ks[7], (DEPTH, N_BRANCH * D_MODEL), 0.01),
        "gmlp_norm_g": 1.0 + nrm(ks[8], (DEPTH, BRANCH_W), 0.02),
        "gmlp_ws": nrm(ks[9], (DEPTH, GMLP_GROUPS, GMLP_CHUNK, GMLP_CHUNK), GMLP_CHUNK ** -0.5),
        "gmlp_bs": 1.0 + nrm(ks[10], (DEPTH, GMLP_GROUPS, GMLP_CHUNK), 0.1),
        "conv_dw": nrm(ks[11], (DEPTH, CONV_W, BRANCH_W), CONV_W ** -0.5),
        "conv_b": nrm(ks[12], (DEPTH, BRANCH_W), 0.01),
        "conv_norm_g": 1.0 + nrm(ks[13], (DEPTH, BRANCH_W), 0.02),
        "q_norm_g": 1.0 + nrm(ks[14], (DEPTH, HEAD_DIM), 0.02),
        "k_norm_g": 1.0 + nrm(ks[15], (DEPTH, HEAD_DIM), 0.02),
        "rel_bias": nrm(ks[16], (DEPTH, N_HEADS, 2 * MAX_REL + 1), 0.1),
        "w_branch": nrm(ks[17], (DEPTH, N_BRANCH, BRANCH_W, D_MODEL), BRANCH_W ** -0.5),
        "w_out": nrm(ks[18], (DEPTH, D_MODEL, D_MODEL), D_MODEL ** -0.5),
        "norm_ffn_g": 1.0 + nrm(ks[19], (DEPTH, D_MODEL), 0.02),
        "w_gate_up": nrm(ks[20], (DEPTH, D_MODEL, 2 * FFN_HIDDEN), D_MODEL ** -0.5),
        "w_down": nrm(ks[21], (DEPTH, FFN_HIDDEN, D_MODEL), FFN_HIDDEN ** -0.5),
    }


def reference(x_prompt, x_sample, cache_attn_k, cache_attn_v, cache_conv, norm_mix_g, w_in, b_gate,
              gmlp_norm_g, gmlp_ws, gmlp_bs, conv_dw, conv_b, conv_norm_g, q_norm_g, k_norm_g, rel_bias,
              w_branch, w_out, norm_ffn_g, w_gate_up, w_down):
    xp = x_prompt
    xs = x_sample
    kp_l, vp_l, cp_l, ks_l, vs_l, cs_l, gs_l = [], [], [], [], [], [], []
    for l in range(DEPTH):
        weights = (norm_mix_g[l], w_in[l], b_gate[l], gmlp_norm_g[l], gmlp_ws[l], gmlp_bs[l],
                   conv_dw[l], conv_b[l], conv_norm_g[l], q_norm_g[l], k_norm_g[l], rel_bias[l],
                   w_branch[l], w_out[l], norm_ffn_g[l], w_gate_up[l], w_down[l])
        xp, nk_p, nv_p, nc_p, _ = trunk_layer(xp, None, None, None, *weights)
        xs, nk_s, nv_s, nc_s, gv_s = trunk_layer(xs, cache_attn_k[l], cache_attn_v[l], cache_conv[l], *weights)
        kp_l.append(nk_p); vp_l.append(nv_p); cp_l.append(nc_p)
        ks_l.append(nk_s); vs_l.append(nv_s); cs_l.append(nc_s); gs_l.append(gv_s)
    return (xp, xs, jnp.stack(kp_l), jnp.stack(vp_l), jnp.stack(cp_l),
            jnp.stack(ks_l), jnp.stack(vs_l), jnp.stack(cs_l), jnp.stack(gs_l))
```

```python
import numpy as np
import concourse.bass as bass
import concourse.mybir as mybir
from concourse.bass_utils import run_bass_kernel_spmd

F32 = mybir.dt.float32
F32R = mybir.dt.float32r
BF16 = mybir.dt.bfloat16
AF = mybir.ActivationFunctionType
ALU = mybir.AluOpType

D = 1024
W = 512
NCOL = 6656
FH = 2816
EPS = 1e-6
SEG = 4096
HALO = 1024
NPC = 182


class Buf:
    __slots__ = ("name", "w", "r", "dsem", "dcount")

    def __init__(self, name):
        self.name = name
        self.w = None
        self.r = {}
        self.dsem = None
        self.dcount = 0


class Sync:
    def __init__(self, nc):
        self.nc = nc
        self.engs = {"pe": nc.tensor, "dve": nc.vector, "act": nc.scalar, "pool": nc.gpsimd, "sp": nc.sync}
        self.sem = {k: nc.alloc_semaphore("s_" + k) for k in self.engs}
        self.cnt = {k: 0 for k in self.engs}
        self.waited = {k: {} for k in self.engs}

    def _need(self, eng, deps):
        e = self.engs[eng]
        best = {}
        for (so, sn, v) in deps:
            if eng == "pe" and sn == "s_pe":
                continue
            if v > best.get(sn, (None, 0))[1]:
                best[sn] = (so, v)
        for sn, (so, v) in best.items():
            if self.waited[eng].get(sn, 0) >= v:
                continue
            e.wait_ge(so, v)
            self.waited[eng][sn] = v

    def _deps(self, reads, writes):
        deps = []
        for b in reads:
            if b.w is not None:
                deps.append(b.w)
        for b in writes:
            if b.w is not None:
                deps.append(b.w)
            deps.extend(b.r.values())
        return deps

    def _mark(self, tag, reads, writes):
        for b in reads:
            b.r[tag[1]] = tag
        for b in writes:
            b.w = tag
            b.r = {}

    def op(self, eng, fn, reads=(), writes=()):
        self._need(eng, self._deps(reads, writes))
        ins = fn(self.engs[eng])
        self.cnt[eng] += 1
        ins.then_inc(self.sem[eng], 1)
        self._mark((self.sem[eng], "s_" + eng, self.cnt[eng]), reads, writes)
        return ins

    def mmg(self, fns, reads=(), writes=()):
        self._need("pe", self._deps(reads, writes))
        ins = None
        for fn in fns:
            ins = fn(self.engs["pe"])
        self.cnt["pe"] += 1
        ins.then_inc(self.sem["pe"], 1)
        self._mark((self.sem["pe"], "s_pe", self.cnt["pe"]), reads, writes)
        return ins

    def dma(self, q, out, in_, reads=(), writes=(), track=None):
        self._need(q, self._deps(reads, writes))
        t = track if track is not None else (writes[0] if writes else reads[0])
        if t.dsem is None:
            t.dsem = self.nc.alloc_semaphore("d_" + t.name)
        ins = self.engs[q].dma_start(out=out, in_=in_)
        ins.then_inc(t.dsem, 16)
        t.dcount += 16
        self._mark((t.dsem, "d_" + t.name, t.dcount), reads, writes)
        return ins

    def finish(self, bufs, eng="sp"):
        deps = []
        for b in bufs:
            if b.w is not None:
                deps.append(b.w)
            deps.extend(b.r.values())
        self._need(eng, deps)


def R(ap):
    return ap


def build_program():
    nc = bass.Bass("TRN2", target_bir_lowering=False)
    S = Sync(nc)

    def din(name, shape):
        return nc.dram_tensor(name, list(shape), F32, kind="ExternalInput").ap()

    def dout(name, shape):
        return nc.dram_tensor(name, list(shape), F32, kind="ExternalOutput").ap()

    xp = din("xp", [HALO + SEG, D])
    xs = din("xs", [128, D])
    ck = din("ck", [2, 512, W])
    cv = din("cv", [2, 512, W])
    cc = din("cc", [2, 30, W])
    flag = din("flag", [128, 1])
    w_in = din("w_in", [2, D, NCOL])
    w_branch = din("w_branch", [2, 3, W, D])
    w_out = din("w_out", [2, D, D])
    w_gu = din("w_gate_up", [2, D, 2 * FH])
    w_down = din("w_down", [2, FH, D])
    pc_d = din("pc", [2, 128, NPC])
    gbc_d = din("gbc", [2, 128, W])
    bsb_d = din("bsb", [2, 128, 4, 128])
    wsT_d = din("wsT", [2, 128, 4, 128])
    relbT_d = din("relbT", [2, 128, 8, 2, 128])
    ident_d = din("ident", [128, 128])
    tril_d = din("trilT", [128, 128])
    bd_d = din("bd", [128, 128])

    y_p = dout("y_p", [SEG, D])
    y_s = dout("y_s", [64, D])
    kp_o = dout("kp", [2, 512, W])
    vp_o = dout("vp", [2, 512, W])
    cp_o = dout("cp", [2, 30, W])
    ks_o = dout("ks", [2, 64, W])
    vs_o = dout("vs", [2, 64, W])
    cs_o = dout("cs", [2, 30, W])
    gs_o = dout("gs", [2, 64, W])
    x1 = nc.dram_tensor("x1_scr", [SEG + 512, D], F32, kind="Internal").ap()
    x1s = nc.dram_tensor("x1s_scr", [128, D], F32, kind="Internal").ap()
    NCH = 38
    wsc = [nc.dram_tensor(f"wsc{l}", [NCH, 128, 4096], BF16, kind="Internal").ap() for l in range(2)]
    _g0 = [Buf(f"wsc0_g{g}") for g in range(8)]
    _g1 = Buf("wsc1")
    b_wsc = [[_g0[min(c // 5, 7)] for c in range(NCH)], [_g1 for c in range(NCH)]]
    b_x1 = [Buf(f"x1blk{j}") for j in range(9)]
    b_x1s = Buf("x1s")

    def sb(name, shape, dt=F32):
        return nc.alloc_sbuf_tensor("sb_" + name, list(shape), dt)

    xts = [sb(f"xt{j}", [128, 4, D]) for j in range(2)]
    b_xts = [[Buf(f"xt{j}_{i}") for i in range(4)] for j in range(2)]
    hT = sb("hT", [128, 8, 512], BF16); b_hT = Buf("hT")
    arena = sb("arena", [128, 22, 512], BF16)
    xnA = sb("xnA", [128, D]); xnB = sb("xnB", [128, D])
    stg = sb("stg", [128, 4, 512]); b_stg = [Buf(f"stg{i}") for i in range(4)]
    stg2 = sb("stg2", [128, 4, 512]); b_stg2 = [Buf(f"stg2_{i}") for i in range(4)]
    tB = [sb(f"tB{i}", [128, 512]) for i in range(2)]; b_tB = [Buf(f"tB{i}") for i in range(2)]
    b_ar = [Buf(f"ar{i}") for i in range(22)]
    gst = sb("gst", [128, 4, 32]); b_gst = Buf("gst")
    gluTb = sb("gluTb", [128, 4, 30 + 512], BF16); b_glub = Buf("gluTb")
    Dg = [sb("Dg0", [128, 16, 128], BF16)] * 2; b_Dg = [Buf("Dg0")] * 2
    identb = sb("identb", [128, 128], BF16); b_idb = Buf("identb")
    pcb = sb("pcb", [128, 124], BF16); b_pcb = Buf("pcb")
    bgh = sb("bgh", [128, 24]); b_bgh = Buf("bgh")
    qT = sb("qT", [128, 4, 512], BF16); b_qT = Buf("qT")
    kT = sb("kT", [128, 4, 1024], BF16); b_k = [Buf(f"kT{i}") for i in range(8)]
    Va = sb("Va", [128, 8, 4, 4, 64], BF16); b_v = [Buf(f"Va{i}") for i in range(8)]
    NWB = 5
    wb = [sb(f"wb{i}", [128, 8, 512], BF16) for i in range(NWB)]
    b_wb = [Buf(f"wb{i}") for i in range(NWB)]
    B8 = sb("B8", [128, 8, 2, 128], BF16); b_B8 = Buf("B8")
    PT = [sb(f"PT{i}", [128, 640], BF16) for i in range(4)]; b_PT = [Buf(f"PT{i}") for i in range(4)]
    tA = [sb(f"tA{i}", [128, 512]) for i in range(2)]; b_tA = [Buf(f"tA{i}") for i in range(2)]
    tR = [sb(f"tR{i}", [128, 512], BF16) for i in range(2)]; b_tR = [Buf(f"tR{i}") for i in range(2)]
    rec = sb("rec", [128, 128]); b_rec = Buf("rec")
    pc = sb("pc", [128, NPC]); b_pc = Buf("pc")
    gbc = sb("gbc", [128, W]); b_gbc = Buf("gbc")
    bsb = sb("bsb", [128, 4, 128]); b_bsb = Buf("bsb")
    WmT = sb("WmT", [128, 4, 128], BF16); b_WmT = Buf("WmT")
    ident = sb("ident", [128, 128]); b_id = Buf("ident")
    trilT = sb("trilT", [128, 128]); b_tril = Buf("trilT")
    bd = sb("bd", [128, 128], BF16); b_bd = Buf("bd")
    ones = sb("ones", [128, 128], BF16); b_ones = Buf("ones")
    onesb = sb("onesb", [128, 4, 2, 64], BF16); b_onesb = Buf("onesb")
    flagb = sb("flagb", [128, 4, 2, 64], BF16); b_flagb = Buf("flagb")
    flg = sb("flg", [128, 1]); b_flg = Buf("flg")
    epsc = sb("epsc", [128, 1]); b_epsc = Buf("epsc")
    sm = sb("sm", [128, 32]); b_sm = Buf("sm")
    cst = sb("cst", [32, W]); b_cst = Buf("cst")

    PA = [nc.alloc_psum_tensor(f"PA{i}", [128, 512], F32) for i in range(2)]; b_PA = [Buf(f"PA{i}") for i in range(2)]
    PB = [nc.alloc_psum_tensor(f"PB{i}", [128, 512], F32) for i in range(2)]; b_PB = [Buf(f"PB{i}") for i in range(2)]
    PC = [nc.alloc_psum_tensor(f"PC{i}", [128, 512], F32) for i in range(4)]; b_PC = [Buf(f"PC{i}") for i in range(4)]
    rot = {"A": 0, "B": 0, "C": 0, "PT": 0, "tA": 0, "tB": 0, "tR": 0}

    def nxt(key, n):
        i = rot[key]
        rot[key] = (i + 1) % n
        return i

    def pa():
        i = nxt("A", 2); return PA[i], b_PA[i]

    def pb():
        i = nxt("B", 2); return PB[i], b_PB[i]

    def pcn():
        i = nxt("C", 4); return PC[i], b_PC[i]

    def tmpA():
        i = nxt("tA", 2); return tA[i], b_tA[i]

    def tmpR():
        i = nxt("tR", 2); return tR[i], b_tR[i]

    def tmpB():
        i = nxt("tB", 2); return tB[i], b_tB[i]

    wq = []
    wstate = {"issued": 0, "taken": 0, "released": 0}

    def w_issue(upto):
        while wstate["issued"] < min(upto, len(wq)):
            n = wstate["issued"]
            name, l_, ci = wq[n]
            bi = n % NWB
            if l_ == 1:
                while conv_pending:
                    convert(*conv_pending.pop(0))
            S.dma("sp", wb[bi][:], wsc[l_][ci].rearrange("p (k c) -> p k c", k=8), reads=[b_wsc[l_][ci]],
                  writes=[b_wb[bi]], track=b_wb[bi])
            wstate["issued"] += 1

    conv_pending = []

    def convert(l_, ci, parts):
        dst = wsc[l_][ci].rearrange("p (k c) -> p k c", k=8)
        for (ap, kc, c0, cols) in parts:
            S.dma("pool", dst[:, 0:kc, c0:c0 + cols], ap.rearrange("(k p) c -> p k c", p=128),
                  writes=[b_wsc[l_][ci]], track=b_wsc[l_][ci])

    def take(name):
        n = wstate["taken"]
        assert wq[n][0] == name, (wq[n][0], name)
        assert n < wstate["released"] + NWB
        w_issue(wstate["released"] + NWB)
        wstate["taken"] += 1
        bi = n % NWB
        return R(wb[bi][:]), b_wb[bi]

    def rel(k=1):
        wstate["released"] += k
        assert wstate["released"] <= wstate["taken"]
        w_issue(wstate["released"] + NWB)
        if conv_pending and wstate["released"] % 2 == 0:
            convert(*conv_pending.pop(0))

    def chunk_list(l, kind):
        out = []
        wi = w_in[l]
        names = ["u", "v", "ga", "gb", "q", "k", "va"]

        def col(nm, c0=0, n=512):
            i = names.index(nm)
            return wi[:, i * 512 + c0:i * 512 + c0 + n]
        for j in range(2):
            out.append((f"gab{j}", [(col("ga", j * 256, 256), 8, 0, 256), (col("gb", j * 256, 256), 8, 256, 256)]))
        use = ["u", "v", "q", "k", "va"] if kind == "full" else ["k", "va"]
        for nm in use:
            out.append((nm, [(col(nm), 8, 0, 512)]))
        if kind != "full":
            return out
        for n in range(3):
            for half in range(2):
                for q in range(2):
                    c0 = 3584 + n * 1024 + half * 512 + q * 256
                    d0 = half * 512 + q * 256
                    out.append((f"gw{n}{half}{q}", [(wi[:, c0:c0 + 256], 8, 0, 256),
                                                    (w_branch[l, n][:, d0:d0 + 256], 4, 256, 256)]))
        for half in range(2):
            out.append((f"wo{half}", [(w_out[l][:, half * 512:(half + 1) * 512], 8, 0, 512)]))
        for i in range(11):
            out.append((f"GU{i}", [(w_gu[l][:, i * 256:(i + 1) * 256], 8, 0, 256),
                                   (w_gu[l][:, FH + i * 256:FH + (i + 1) * 256], 8, 256, 256)]))
        for half in range(2):
            for j in range(3):
                r0 = j * 1024
                r1 = min(r0 + 1024, FH)
                out.append((f"D{half}{j}", [(w_down[l][r0:r1, half * 512:(half + 1) * 512], (r1 - r0) // 128, 0, 512)]))
        return out

    sched = []
    for l in range(2):
        n = 0
        first = 0 if l == 0 else 1
        for i in range(first, 10):
            kind = "h" if i == first else "full"
            sb_, db_ = [], []
            if l == 0:
                src = xp[i * 512:(i + 1) * 512, :]
                dst = x1[(i - 1) * 512:i * 512, :] if i >= 1 else None
                if i >= 1:
                    db_ = [b_x1[i - 1]]
            else:
                src = x1[(i - 1) * 512:i * 512, :]
                sb_ = [b_x1[i - 1]]
                dst = y_p[(i - 2) * 512:(i - 1) * 512, :] if i >= 2 else None
            sched.append(dict(l=l, kind=kind, NS=4, src=src, dst=dst, base=(n % 2) * 4, last=(i == 9), sample=False,
                              postflag=(l == 0 and i == 1), srcb=sb_, dstb=db_))
            n += 1
        sched.append(dict(l=l, kind="full", NS=1, src=(xs if l == 0 else x1s), dst=(x1s if l == 0 else y_s),
                          base=4, last=False, sample=True, postflag=False,
                          srcb=([] if l == 0 else [b_x1s]), dstb=([b_x1s] if l == 0 else [])))
    chunk_idx = [{}, {}]
    conv0 = []
    for l in range(2):
        full_list = chunk_list(l, "full")
        assert len(full_list) == NCH
        for ci, (nm, parts) in enumerate(full_list):
            chunk_idx[l][nm] = ci
            if l == 0:
                conv0.append((0, ci, parts))
            else:
                conv_pending.append((1, ci, parts))
    for ti, t in enumerate(sched):
        t["xb"] = ti % 2
        for (nm, parts) in chunk_list(t["l"], t["kind"]):
            wq.append((nm, t["l"], chunk_idx[t["l"]][nm]))

    S.dma("pool", ident[:], ident_d, writes=[b_id])
    S.dma("pool", trilT[:], tril_d, writes=[b_tril])
    S.dma("pool", flg[:], flag, writes=[b_flg])
    S.dma("pool", tA[1][:, 0:128], bd_d, writes=[b_tA[1]])
    S.op("dve", lambda e: e.tensor_copy(bd[:], tA[1][:, 0:128]), reads=[b_tA[1]], writes=[b_bd])
    S.op("dve", lambda e: e.memset(ones[:], 1.0), writes=[b_ones])
    S.op("dve", lambda e: e.memset(onesb[:], 1.0), writes=[b_onesb])
    S.op("dve", lambda e: e.memset(epsc[:], EPS), writes=[b_epsc])
    S.op("dve", lambda e: e.tensor_scalar(out=flagb[:].rearrange("p a b c -> p (a b c)"),
                                          in0=onesb[:].rearrange("p a b c -> p (a b c)"),
                                          scalar1=flg[:, 0:1], scalar2=None, op0=ALU.mult),
         reads=[b_onesb, b_flg], writes=[b_flagb])
    S.op("dve", lambda e: e.memset(gluTb[:], 0.0), writes=[b_glub])
    S.op("dve", lambda e: e.tensor_copy(identb[:], ident[:]), reads=[b_id], writes=[b_idb])
    S.op("dve", lambda e: e.memset(kT[:], 0.0), writes=b_k)
    S.op("dve", lambda e: e.memset(Va[:].rearrange("p a b c d -> p (a b c d)"), 0.0), writes=b_v)

    C_GMIX, C_GFFN, C_BG, C_CB, C_CG, C_DW, C_GQ, C_GK, C_B256 = 0, 8, 16, 40, 44, 48, 172, 173, 174
    SM_NB, SM_GKF, SM_SS, SM_RS = 0, 10, 12, 16
    SM_SS2, SM_RS2 = 20, 24
    b_sm2 = Buf("sm2")

    def layer_setup(l):
        S.dma("pool", pc[:], pc_d[l], writes=[b_pc])
        S.dma("pool", gbc[:], gbc_d[l], writes=[b_gbc])
        S.dma("pool", bsb[:], bsb_d[l], writes=[b_bsb])
        S.dma("pool", tA[0][:], wsT_d[l].rearrange("p a b -> p (a b)"), writes=[b_tA[0]])
        S.dma("pool", stg[:].rearrange("p a b -> p (a b)"), relbT_d[l].rearrange("p h j q -> p (h j q)"),
              writes=b_stg, track=b_stg[0])
        for g in range(4):
            S.op("dve", lambda e, g=g: e.tensor_tensor(out=R(WmT[:, g, :]), in0=tA[0][:, g * 128:(g + 1) * 128], in1=trilT[:], op=ALU.mult),
                 reads=[b_tA[0], b_tril], writes=[b_WmT])
        S.op("dve", lambda e: e.tensor_copy(pcb[:], pc[:, C_DW:C_DW + 124]), reads=[b_pc], writes=[b_pcb])
        S.op("dve", lambda e: e.tensor_scalar(out=bgh[:], in0=pc[:, C_BG:C_BG + 24], scalar1=0.5, scalar2=None, op0=ALU.mult),
             reads=[b_pc], writes=[b_bgh])
        S.op("dve", lambda e: e.tensor_scalar(out=sm[:, SM_NB:SM_NB + 8], in0=pc[:, C_B256:C_B256 + 8], scalar1=-1.0,
                                              scalar2=None, op0=ALU.mult), reads=[b_pc], writes=[b_sm])
        S.op("dve", lambda e: e.tensor_scalar(out=sm[:, SM_GKF:SM_GKF + 1], in0=pc[:, C_GK:C_GK + 1], scalar1=flg[:, 0:1],
                                              scalar2=None, op0=ALU.mult), reads=[b_pc, b_flg], writes=[b_sm])
        stgf = stg[:].rearrange("p a b -> p (a b)")
        for h in range(8):
            S.op("dve", lambda e, h=h: e.tensor_scalar(out=B8[:, h].rearrange("p a b -> p (a b)"),
                                                       in0=stgf[:, h * 256:(h + 1) * 256],
                                                       scalar1=sm[:, SM_NB + h:SM_NB + h + 1], scalar2=8.0,
                                                       op0=ALU.add, op1=ALU.mult),
                 reads=b_stg + [b_sm], writes=[b_B8])

    xnb = [(xnA[:], [Buf("xnA")]), (xnB[:], [Buf("xnB")])]

    def norm_to_hT(NS, goff, xt, b_xt):
        for c in norm_closures(NS, goff, xt, b_xt):
            c()

    def norm_closures(NS, goff, xt, b_xt):
        def stats():
            for s in range(NS):
                xn_, bxn_ = xnb[s % 2]
                S.op("act", lambda e: e.activation(R(xn_), xt[:, s, :], AF.Square, accum_out=sm[:, SM_SS2 + s:SM_SS2 + s + 1]),
                     reads=[b_xt[s]], writes=bxn_ + [b_sm2])
            S.op("act", lambda e: e.activation(sm[:, SM_RS2:SM_RS2 + NS], sm[:, SM_SS2:SM_SS2 + NS], AF.Sqrt,
                                               scale=1.0 / D, bias=EPS), reads=[b_sm2], writes=[b_sm2])
            S.op("dve", lambda e: e.reciprocal(sm[:, SM_RS2:SM_RS2 + NS], sm[:, SM_RS2:SM_RS2 + NS]), reads=[b_sm2], writes=[b_sm2])
        return [stats] + [(lambda s=s: norm_part(s, goff, xt, b_xt)) for s in range(NS)]

    def norm_part(s, goff, xt, b_xt):
        if True:
            xn_, bxn_ = xnb[s % 2]
            S.op("act", lambda e: e.activation(R(xn_), xt[:, s, :], AF.Copy, scale=sm[:, SM_RS2 + s:SM_RS2 + s + 1]),
                 reads=[b_xt[s], b_sm2], writes=bxn_)
            for kk in range(2):
                p_, bp_ = pa() if kk == 0 else pb()
                S.mmg([(lambda e, j=j: e.transpose(p_[:, j * 128:(j + 1) * 128],
                                                   xn_[:, (kk * 4 + j) * 128:(kk * 4 + j + 1) * 128], ident[:]))
                       for j in range(4)], reads=bxn_ + [b_id], writes=[bp_])
                S.op("dve", lambda e: e.tensor_tensor(
                    out=R(hT[:, kk * 4:kk * 4 + 4, s * 128:(s + 1) * 128]),
                    in0=p_[:, :].rearrange("p (j t) -> p j t", j=4),
                    in1=pc[:, goff + kk * 4:goff + kk * 4 + 4].unsqueeze(2).to_broadcast([128, 4, 128]), op=ALU.mult),
                     reads=[bp_, b_pc], writes=[b_hT])

    def proj_fm(dst, bdst, Wt, bW, cg, T):
        S.mmg([(lambda e, k=k: e.matmul(dst[:, 0:T], lhsT=Wt[:, k, cg * 128:(cg + 1) * 128], rhs=R(hT[:, k, 0:T]),
                                        start=(k == 0), stop=(k == 7))) for k in range(8)],
              reads=[bW, b_hT], writes=[bdst])

    def proj_tm(dst, bdst, Wt, bW, s):
        S.mmg([(lambda e, k=k: e.matmul(dst[:, 0:512], lhsT=R(hT[:, k, s * 128:(s + 1) * 128]), rhs=Wt[:, k, 0:512],
                                        start=(k == 0), stop=(k == 7))) for k in range(8)],
              reads=[bW, b_hT], writes=[bdst])

    def transpose_out(src_ap_fn, rows, ncols_tok, dst_tile, bsrc, bdst, dcol0=0):
        for cg in range(4):
            p_, bp_ = pa()
            S.mmg([lambda e: e.transpose(p_[0:ncols_tok, 0:128], src_ap_fn(cg), ident[:])], reads=[bsrc, b_id], writes=[bp_])
            S.op("dve", lambda e: e.tensor_copy(dst_tile[0:ncols_tok, cg * 128:(cg + 1) * 128], p_[0:ncols_tok, 0:128]),
                 reads=[bp_], writes=[bdst])

    def qk_norm(Wt, bW, T, gcol_ap, gcol_bufs, out_fn, out_bufs, f32_out=None, after=None):
        for cg in range(4):
            p_, bp_ = pa()
            proj_fm(p_, bp_, Wt, bW, cg, T)
            qs, bqs = tmpA()
            sr, bsr = tmpR()
            sq, bsq = tmpB()
            S.op("act", lambda e: e.activation(qs[:, 0:T], p_[:, 0:T], AF.Copy), reads=[bp_], writes=[bqs])
            S.op("act", lambda e: e.activation(R(sr[:, 0:T]), p_[:, 0:T], AF.Square), reads=[bp_], writes=[bsr])
            p2, bp2 = pb()
            S.mmg([lambda e: e.matmul(p2[:, 0:T], lhsT=R(bd[:]), rhs=R(sr[:, 0:T]), start=True, stop=True)],
                  reads=[bsr, b_bd], writes=[bp2])
            S.op("act", lambda e: e.activation(R(sq[:, 0:T]), p2[:, 0:T], AF.Ln, scale=1.0 / 64, bias=epsc[:, 0:1]),
                 reads=[bp2, b_epsc], writes=[bsq])
            S.op("act", lambda e: e.activation(R(sq[:, 0:T]), sq[:, 0:T], AF.Exp, scale=-0.5), reads=[bsq], writes=[bsq])
            S.op("dve", lambda e: e.scalar_tensor_tensor(out=out_fn(cg), in0=qs[:, 0:T], scalar=gcol_ap, in1=sq[:, 0:T],
                                                         op0=ALU.mult, op1=ALU.mult),
                 reads=[bqs, bsq] + gcol_bufs, writes=out_bufs)
            if f32_out is not None:
                f32_out(cg, qs, bqs, sq, bsq)
            if after is not None:
                after()

    def load_x(t):
        if t.get("loaded"):
            return
        t["loaded"] = True
        for s in range(t["NS"]):
            S.dma("sp", xts[t["xb"]][:, s, :], t["src"][s * 128:(s + 1) * 128, :], reads=t["srcb"],
                  writes=[b_xts[t["xb"]][s]], track=b_xts[t["xb"]][s])

    def run_tile(t, t_next):
        l, kind, NS, base = t["l"], t["kind"], t["NS"], t["base"]
        xt, b_xt = xts[t["xb"]], b_xts[t["xb"]]
        T = NS * 128
        full = kind == "full"
        sample = t["sample"]
        want_out = t["last"] or sample
        slots = [(base + s) % 8 for s in range(NS)]
        bg = []

        def drain(n=1):
            for _ in range(n):
                if bg:
                    bg.pop(0)()

        if sample:
            for s4 in range(4):
                S.dma("pool", stg[:, s4, :], ck[l, s4 * 128:(s4 + 1) * 128, :], writes=[b_stg[s4]])
                S.dma("pool", stg2[:, s4, :], cv[l, s4 * 128:(s4 + 1) * 128, :], writes=[b_stg2[s4]])
            S.dma("pool", cst[0:30, :], cc[l], writes=[b_cst])
            for s4 in range(4):
                for cg in range(4):
                    p_, bp_ = pa()
                    S.mmg([lambda e: e.transpose(p_[:, 0:128], stg[:, s4, cg * 128:(cg + 1) * 128], ident[:])],
                          reads=[b_stg[s4], b_id], writes=[bp_])
                    S.op("dve", lambda e: e.tensor_copy(kT[:, cg, s4 * 128:(s4 + 1) * 128], p_[:, 0:128]),
                         reads=[bp_], writes=[b_k[s4]])
                S.op("dve", lambda e: e.tensor_copy(Va[:, s4, :, 0:4:3, :],
                                                    stg2[:, s4, :].rearrange("p (a b c) -> p a b c", a=4, b=2)),
                     reads=[b_stg2[s4]], writes=[b_v[s4]])
                S.op("dve", lambda e: e.tensor_copy(Va[:, s4, :, 1:3, :], onesb[:]), reads=[b_onesb], writes=[b_v[s4]])
            for cg in range(4):
                p_, bp_ = pa()
                S.mmg([lambda e: e.transpose(p_[:, 0:30], cst[0:30, cg * 128:(cg + 1) * 128], ident[0:30, 0:30])],
                      reads=[b_cst, b_id], writes=[bp_])
                S.op("act", lambda e: e.activation(gluTb[:, cg, 0:30], p_[:, 0:30], AF.Copy), reads=[bp_], writes=[b_glub])
        elif full:
            S.op("dve", lambda e: e.tensor_copy(gluTb[:, :, 0:30], gluTb[:, :, 512:542]), reads=[b_glub], writes=[b_glub])
        load_x(t)
        if not t.get("normed"):
            norm_to_hT(NS, C_GMIX, xt, b_xt)

        fl = flg[:, 0:1]
        for cg in range(4):
            if cg % 2 == 0:
                if cg:
                    rel()
                Wa, bWa = take(f"gab{cg // 2}")
            p1, bp1 = pa()
            p2, bp2 = pb()
            proj_fm(p1, bp1, Wa, bWa, cg % 2, T)
            proj_fm(p2, bp2, Wa, bWa, 2 + cg % 2, T)
            ta, bta = tmpA()
            S.op("act", lambda e: e.activation(ta[:, 0:T], p2[:, 0:T], AF.Sigmoid), reads=[bp2], writes=[bta])
            if full:
                S.op("dve", lambda e: e.tensor_tensor(out=gluTb[:, cg, 30:30 + T], in0=p1[:, 0:T], in1=ta[:, 0:T], op=ALU.mult),
                     reads=[bp1, bta], writes=[b_glub])
            else:
                S.op("dve", lambda e: e.scalar_tensor_tensor(out=gluTb[:, cg, 30:30 + T], in0=p1[:, 0:T], scalar=fl,
                                                             in1=ta[:, 0:T], op0=ALU.mult, op1=ALU.mult),
                     reads=[bp1, bta, b_flg], writes=[b_glub])
            if want_out:
                o0 = (T - 30) if not sample else 34
                S.op("dve", lambda e: e.tensor_tensor(out=gst[:, cg, 0:30], in0=p1[:, o0:o0 + 30], in1=ta[:, o0:o0 + 30],
                                                      op=ALU.mult), reads=[bp1, bta], writes=[b_gst])
        rel()
        if want_out:
            transpose_out(lambda cg: gst[:, cg, 0:30], 128, 30, cst, b_gst, b_cst)
            S.dma("pool", (cs_o if sample else cp_o)[l], cst[0:30, :], reads=[b_cst])
        if full:
            for cg in range(4):
                for half in range(2):
                    def conv_part(cg=cg, half=half):
                        k0 = half * 16
                        nk = 16 if half == 0 else 15
                        d, bd_ = Dg[half], b_Dg[half]
                        c0 = cg * 31 + k0
                        S.op("dve", lambda e: e.tensor_tensor(out=d[:, 0:nk, :],
                                                              in0=identb[:].unsqueeze(1).to_broadcast([128, nk, 128]),
                                                              in1=pcb[:, c0:c0 + nk].unsqueeze(2).to_broadcast([128, nk, 128]),
                                                              op=ALU.mult),
                             reads=[b_idb, b_pcb], writes=[bd_])
                        S.mmg([(lambda e, i=i: e.matmul(PC[cg][:, 0:T], lhsT=d[:, i, :], rhs=gluTb[:, cg, k0 + i:k0 + i + T],
                                                        start=(k0 + i == 0), stop=(k0 + i == 30))) for i in range(nk)],
                              reads=[bd_, b_glub], writes=[b_PC[cg]])
                    bg.append(conv_part)

                def conv_evac(cg=cg):
                    cb = pc[:, C_CB + cg:C_CB + cg + 1]
                    S.op("dve", lambda e: e.tensor_scalar(out=R(arena[:, 8 + cg, 0:T]), in0=PC[cg][:, 0:T], scalar1=cb,
                                                          scalar2=None, op0=ALU.add),
                         reads=[b_PC[cg], b_pc], writes=[b_ar[8 + cg]])
                    S.op("act", lambda e: e.activation(R(arena[:, 12 + cg, 0:T]), PC[cg][:, 0:T], AF.Square, bias=cb, scale=1.0),
                         reads=[b_PC[cg], b_pc], writes=[b_ar[12 + cg]])
                bg.append(conv_evac)

        if full:
            Wt, bW = take("u")
            for cg in range(4):
                p_, bp_ = pa()
                proj_fm(p_, bp_, Wt, bW, cg, T)
                S.op("act", lambda e: e.activation(R(arena[:, cg, 0:T]), p_[:, 0:T], AF.Gelu), reads=[bp_], writes=[b_ar[cg]])
                drain()
            rel()
            Wt, bW = take("v")
            for s in range(NS):
                p_, bp_ = pa()
                proj_tm(p_, bp_, Wt, bW, s)
                vt = stg2[:, s, :]
                S.op("act", lambda e: e.activation(vt, p_[:, :], AF.Gelu), reads=[bp_], writes=[b_stg2[s]])
                drain()
            rel()
            for s in range(NS):
                vt = stg2[:, s, :]
                ta, bta = tmpA()
                S.op("act", lambda e: e.activation(ta[:], vt, AF.Square, accum_out=sm[:, SM_SS + s:SM_SS + s + 1]),
                     reads=[b_stg2[s]], writes=[bta, b_sm])
            S.op("act", lambda e: e.activation(sm[:, SM_RS:SM_RS + NS], sm[:, SM_SS:SM_SS + NS], AF.Sqrt,
                                               scale=1.0 / W, bias=EPS), reads=[b_sm], writes=[b_sm])
            S.op("dve", lambda e: e.reciprocal(sm[:, SM_RS:SM_RS + NS], sm[:, SM_RS:SM_RS + NS]), reads=[b_sm], writes=[b_sm])
            for s in range(NS):
                vt = stg2[:, s, :]
                S.op("dve", lambda e: e.scalar_tensor_tensor(out=R(arena[:, 4 + s, :]), in0=vt, scalar=sm[:, SM_RS + s:SM_RS + s + 1],
                                                             in1=gbc[:], op0=ALU.mult, op1=ALU.mult),
                     reads=[b_stg2[s], b_sm, b_gbc], writes=[b_ar[4 + s]])
                if sample:
                    S.op("dve", lambda e: e.scalar_tensor_tensor(out=vt, in0=vt, scalar=sm[:, SM_RS + s:SM_RS + s + 1],
                                                                 in1=gbc[:], op0=ALU.mult, op1=ALU.mult),
                         reads=[b_stg2[s], b_sm, b_gbc], writes=[b_stg2[s]])
                    S.dma("pool", gs_o[l], stg2[0:64, s, :], reads=[b_stg2[s]])
            Wt, bW = take("q")
            qk_norm(Wt, bW, T, pc[:, C_GQ:C_GQ + 1], [b_pc], lambda cg: qT[:, cg, 0:T], [b_qT], after=drain)
            rel()
        Wt, bW = take("k")
        kbufs = [b_k[sl] for sl in slots]

        def k_f32(cg, qs, bqs, sq, bsq):
            S.op("dve", lambda e: e.scalar_tensor_tensor(out=qs[:, 0:T], in0=qs[:, 0:T], scalar=pc[:, C_GK:C_GK + 1],
                                                         in1=sq[:, 0:T], op0=ALU.mult, op1=ALU.mult),
                 reads=[bqs, bsq, b_pc], writes=[bqs])
            for s in range(NS):
                p_, bp_ = pb()
                S.mmg([lambda e: e.transpose(p_[:, 0:128], qs[:, s * 128:(s + 1) * 128], ident[:])],
                      reads=[bqs, b_id], writes=[bp_])
                S.op("dve", lambda e: e.tensor_copy(stg[:, s, cg * 128:(cg + 1) * 128], p_[:, 0:128]),
                     reads=[bp_], writes=[b_stg[s]])

        gk_ap = pc[:, C_GK:C_GK + 1] if full else sm[:, SM_GKF:SM_GKF + 1]
        qk_norm(Wt, bW, T, gk_ap, [b_pc, b_sm], lambda cg: kT[:, cg, base * 128:base * 128 + T], kbufs,
                f32_out=(k_f32 if want_out else None), after=drain)
        rel()
        if want_out:
            for s in range(NS):
                if sample:
                    S.dma("pool", ks_o[l], stg[0:64, s, :], reads=[b_stg[s]])
                else:
                    S.dma("pool", kp_o[l, s * 128:(s + 1) * 128, :], stg[:, s, :], reads=[b_stg[s]])
        Wt, bW = take("va")
        for s in range(NS):
            sl = slots[s]
            p_, bp_ = pa()
            proj_tm(p_, bp_, Wt, bW, s)
            pv = p_[:, :].rearrange("p (a b c) -> p a b c", a=4, b=2)
            if full:
                S.op("act", lambda e: e.activation(Va[:, sl, :, 0:4:3, :], pv, AF.Copy), reads=[bp_], writes=[b_v[sl]])
                S.op("dve", lambda e: e.tensor_copy(Va[:, sl, :, 1:3, :], onesb[:]), reads=[b_onesb], writes=[b_v[sl]])
            else:
                S.op("act", lambda e: e.activation(Va[:, sl, :, 0:4:3, :], pv, AF.Copy, scale=fl), reads=[bp_, b_flg],
                     writes=[b_v[sl]])
                S.op("dve", lambda e: e.tensor_copy(Va[:, sl, :, 1:3, :], flagb[:]), reads=[b_flagb], writes=[b_v[sl]])
            if want_out:
                S.op("dve", lambda e: e.tensor_copy(stg2[:, s, :], p_[:, :]), reads=[bp_], writes=[b_stg2[s]])
                if sample:
                    S.dma("pool", vs_o[l], stg2[0:64, s, :], reads=[b_stg2[s]])
                else:
                    S.dma("pool", vp_o[l, s * 128:(s + 1) * 128, :], stg2[:, s, :], reads=[b_stg2[s]])
            drain()
        rel()
        if t_next is not None:
            load_x(t_next)
        if not full:
            return
        drain(len(bg))

        p2, bp2 = pb()
        S.mmg([(lambda e, cg=cg: e.matmul(p2[:, 0:T], lhsT=R(ones[:]), rhs=R(arena[:, 12 + cg, 0:T]), start=(cg == 0),
                                          stop=(cg == 3))) for cg in range(4)],
              reads=[b_ar[12 + cg] for cg in range(4)] + [b_ones], writes=[bp2])
        rs_, brs_ = tmpR()
        S.op("act", lambda e: e.activation(R(rs_[:, 0:T]), p2[:, 0:T], AF.Ln, scale=1.0 / W, bias=epsc[:, 0:1]),
             reads=[bp2, b_epsc], writes=[brs_])
        S.op("act", lambda e: e.activation(R(rs_[:, 0:T]), rs_[:, 0:T], AF.Exp, scale=-0.5), reads=[brs_], writes=[brs_])
        for cg in range(4):
            acc = arena[:, 8 + cg, 0:T]
            ta, bta = tmpA()
            S.op("dve", lambda e: e.tensor_tensor(out=ta[:, 0:T], in0=acc, in1=rs_[:, 0:T], op=ALU.mult),
                 reads=[b_ar[8 + cg], brs_], writes=[bta])
            S.op("act", lambda e: e.activation(R(acc), ta[:, 0:T], AF.Silu, scale=pc[:, C_CG + cg:C_CG + cg + 1]),
                 reads=[bta, b_pc], writes=[b_ar[8 + cg]])
        for g in range(4):
            p_, bp_ = pa()
            S.mmg([(lambda e, s=s: e.matmul(p_[:, s * 128:(s + 1) * 128], lhsT=R(arena[:, 4 + s, g * 128:(g + 1) * 128]),
                                            rhs=R(WmT[:, g, :]), start=True, stop=True)) for s in range(NS)],
                  reads=[b_ar[4 + s] for s in range(NS)] + [b_WmT], writes=[bp_])
            ta, bta = tmpA()
            for s in range(NS):
                S.op("dve", lambda e, s=s: e.tensor_tensor(out=ta[:, s * 128:(s + 1) * 128], in0=p_[:, s * 128:(s + 1) * 128],
                                                           in1=bsb[:, g, :], op=ALU.add), reads=[bp_, b_bsb], writes=[bta])
            S.op("dve", lambda e: e.tensor_tensor(out=R(arena[:, g, 0:T]), in0=arena[:, g, 0:T], in1=ta[:, 0:T], op=ALU.mult),
                 reads=[b_ar[g], bta], writes=[b_ar[g]])

        def attn_p1(u, s, p):
            wslots = [(base + s - 4 + j) % 8 for j in range(5)]
            X, bX = PC[2 + u % 2], b_PC[2 + u % 2]
            for e2 in range(2):
                h = 2 * p + e2
                lo, hi = 64 * e2, 64 * e2 + 64
                Sa, bSa = PC[e2], b_PC[e2]
                qs_ = qT[lo:hi, p, s * 128:(s + 1) * 128]
                fns = []
                for j in range(4):
                    fns.append(lambda e, j=j: e.matmul(Sa[:, j * 128:(j + 1) * 128],
                                                       lhsT=kT[lo:hi, p, wslots[j] * 128:(wslots[j] + 1) * 128],
                                                       rhs=qs_, start=True, stop=(j < 3)))
                fns.append(lambda e: e.matmul(Sa[:, 384:512], lhsT=identb[:], rhs=B8[:, h, 0, :], start=False, stop=True))
                S.mmg(fns, reads=[b_k[wslots[j]] for j in range(4)] + [b_qT, b_idb, b_B8], writes=[bSa])
                xs4 = X[:, 256 + e2 * 128:256 + (e2 + 1) * 128]
                S.mmg([lambda e: e.matmul(xs4, lhsT=kT[lo:hi, p, wslots[4] * 128:(wslots[4] + 1) * 128], rhs=qs_,
                                          start=True, stop=False),
                       lambda e: e.matmul(xs4, lhsT=identb[:], rhs=B8[:, h, 1, :], start=False, stop=True)],
                      reads=[b_k[wslots[4]], b_qT, b_idb, b_B8], writes=[bX])
                ip = (u % 2) * 2 + e2
                P_, bP_ = PT[ip], b_PT[ip]
                S.op("act", lambda e: e.activation(P_[:, 0:512], Sa[:, :], AF.Exp, scale=0.125), reads=[bSa], writes=[bP_])
                S.op("act", lambda e: e.activation(P_[:, 512:640], xs4, AF.Exp, scale=0.125), reads=[bX], writes=[bP_])
                S.op("dve", lambda e: e.memset(P_[0:64, 64:128], 0.0), writes=[bP_])

        def attn_p2(u, s, p):
            wslots = [(base + s - 4 + j) % 8 for j in range(5)]
            X, bX = PC[2 + u % 2], b_PC[2 + u % 2]
            for e2 in range(2):
                ip = (u % 2) * 2 + e2
                P_, bP_ = PT[ip], b_PT[ip]
                S.mmg([(lambda e, j=j: e.matmul(X[:, e2 * 128:(e2 + 1) * 128],
                                                lhsT=Va[:, wslots[j], p, 2 * e2:2 * e2 + 2, :].rearrange("p a b -> p (a b)"),
                                                rhs=P_[:, j * 128:(j + 1) * 128], start=(j == 0), stop=(j == 4)))
                       for j in range(5)],
                      reads=[b_v[wslots[j]] for j in range(5)] + [bP_], writes=[bX])
            S.op("dve", lambda e: e.reciprocal(rec[0:64, :], X[64:128, 0:128]), reads=[bX], writes=[b_rec])
            S.op("dve", lambda e: e.reciprocal(rec[64:128, :], X[0:64, 128:256]), reads=[bX], writes=[b_rec])
            S.op("dve", lambda e: e.tensor_tensor(out=R(arena[0:64, 12 + p, s * 128:(s + 1) * 128]), in0=X[0:64, 0:128],
                                                  in1=rec[0:64, :], op=ALU.mult), reads=[bX, b_rec], writes=[b_ar[12 + p]])
            S.op("dve", lambda e: e.tensor_tensor(out=R(arena[64:128, 12 + p, s * 128:(s + 1) * 128]), in0=X[64:128, 128:256],
                                                  in1=rec[64:128, :], op=ALU.mult), reads=[bX, b_rec], writes=[b_ar[12 + p]])

        units = [(s, p) for s in range(NS) for p in range(4)]
        for u in range(len(units) + 1):
            def step(u=u):
                if u < len(units):
                    attn_p1(u, *units[u])
                if u >= 1:
                    attn_p2(u - 1, *units[u - 1])
            bg.append(step)

        mslots = [4, 5, 6, 7, 16, 17, 18, 19]
        ysl = [0, 8, 12]
        for n in range(3):
            if n == 2:
                drain(len(bg))
            for half in range(2):
                for ee in range(4):
                    if ee % 2 == 0:
                        if ee:
                            rel()
                        Wg, bWg = take(f"gw{n}{half}{ee // 2}")
                        Wbr, bWbr = Wg, bWg
                    e_ = half * 4 + ee
                    ms = mslots[e_]
                    p2, bp2 = pb()
                    proj_fm(p2, bp2, Wg, bWg, ee % 2, T)
                    gt, bgt = tmpA()
                    S.op("act", lambda e: e.activation(gt[:, 0:T], p2[:, 0:T], AF.Tanh,
                                                       bias=bgh[:, n * 8 + e_:n * 8 + e_ + 1], scale=0.5),
                         reads=[bp2, b_bgh], writes=[bgt])
                    p1, bp1 = pa()
                    S.mmg([(lambda e, k=k: e.matmul(p1[:, 0:T], lhsT=Wbr[:, k, 256 + (ee % 2) * 128:256 + (ee % 2 + 1) * 128],
                                                    rhs=R(arena[:, ysl[n] + k, 0:T]), start=(k == 0), stop=(k == 3)))
                           for k in range(4)],
                          reads=[bWbr] + [b_ar[ysl[n] + k] for k in range(4)], writes=[bp1])
                    if n == 0:
                        S.op("dve", lambda e: e.scalar_tensor_tensor(out=R(arena[:, ms, 0:T]), in0=gt[:, 0:T], scalar=1.0,
                                                                     in1=p1[:, 0:T], op0=ALU.add, op1=ALU.mult),
                             reads=[bp1, bgt], writes=[b_ar[ms]])
                    else:
                        S.op("dve", lambda e: e.scalar_tensor_tensor(out=gt[:, 0:T], in0=gt[:, 0:T], scalar=1.0,
                                                                     in1=p1[:, 0:T], op0=ALU.add, op1=ALU.mult),
                             reads=[bp1, bgt], writes=[bgt])
                        S.op("dve", lambda e: e.tensor_tensor(out=R(arena[:, ms, 0:T]), in0=arena[:, ms, 0:T], in1=gt[:, 0:T],
                                                              op=ALU.add),
                             reads=[b_ar[ms], bgt], writes=[b_ar[ms]])
                    drain()
                rel()
        for half in range(2):
            Wt, bW = take(f"wo{half}")
            for s in range(NS):
                p1, bp1 = pa()
                S.mmg([(lambda e, k=k: e.matmul(p1[:, :], lhsT=R(arena[:, mslots[k], s * 128:(s + 1) * 128]), rhs=Wt[:, k, :],
                                                start=(k == 0), stop=(k == 7))) for k in range(8)],
                      reads=[bW] + [b_ar[m] for m in mslots], writes=[bp1])
                S.op("dve", lambda e: e.scalar_tensor_tensor(out=xt[:, s, half * 512:(half + 1) * 512], in0=p1[:, :], scalar=0.5,
                                                             in1=xt[:, s, half * 512:(half + 1) * 512], op0=ALU.mult, op1=ALU.add),
                     reads=[b_xt[s], bp1], writes=[b_xt[s]])
            rel()

        norm_to_hT(NS, C_GFFN, xt, b_xt)

        def ffn_cols(Wg, bWg, gcg, Wu, bWu, ucg, c):
            p1, bp1 = pa()
            p2, bp2 = pb()
            proj_fm(p1, bp1, Wg, bWg, gcg, T)
            proj_fm(p2, bp2, Wu, bWu, ucg, T)
            ta, bta = tmpA()
            S.op("act", lambda e: e.activation(ta[:, 0:T], p1[:, 0:T], AF.Silu), reads=[bp1], writes=[bta])
            S.op("dve", lambda e: e.tensor_tensor(out=R(arena[:, c, 0:T]), in0=p2[:, 0:T], in1=ta[:, 0:T], op=ALU.mult),
                 reads=[bp2, bta], writes=[b_ar[c]])

        for i in range(11):
            Wg, bWg = take(f"GU{i}")
            for cg in range(2):
                ffn_cols(Wg, bWg, cg, Wg, bWg, 2 + cg, 2 * i + cg)
            rel()
        if t_next is not None and t_next["l"] == l:
            t_next["normed"] = True
            bg.extend(norm_closures(t_next["NS"], C_GMIX, xts[t_next["xb"]], b_xts[t_next["xb"]]))
        for half in range(2):
            for j in range(3):
                Wt, bW = take(f"D{half}{j}")
                kc = 8 if j < 2 else 6
                for s in range(NS):
                    S.mmg([(lambda e, k=k: e.matmul(PC[s][:, :], lhsT=R(arena[:, 8 * j + k, s * 128:(s + 1) * 128]),
                                                    rhs=Wt[:, k, :], start=(j == 0 and k == 0), stop=(j == 2 and k == kc - 1)))
                           for k in range(kc)],
                          reads=[bW] + [b_ar[8 * j + k] for k in range(kc)], writes=[b_PC[s]])
                rel()
                drain()
            for s in range(NS):
                S.op("dve", lambda e: e.tensor_tensor(out=xt[:, s, half * 512:(half + 1) * 512],
                                                      in0=xt[:, s, half * 512:(half + 1) * 512], in1=PC[s][:, :], op=ALU.add),
                     reads=[b_xt[s], b_PC[s]], writes=[b_xt[s]])
        drain(len(bg))
        if t["postflag"]:
            for s in range(NS):
                S.op("dve", lambda e: e.tensor_copy(Va[:, slots[s], :, 1:3, :], flagb[:]), reads=[b_flagb],
                     writes=[b_v[slots[s]]])
        dst = t["dst"]
        for s in range(NS):
            if sample and l == 1:
                S.dma("pool", dst, xt[0:64, s, :], reads=[b_xt[s]], writes=t["dstb"], track=b_xt[s])
            else:
                S.dma("pool", dst[s * 128:(s + 1) * 128, :], xt[:, s, :], reads=[b_xt[s]], writes=t["dstb"], track=b_xt[s])

    cur_l = -1
    for ti, t in enumerate(sched):
        if t["l"] != cur_l:
            cur_l = t["l"]
            layer_setup(cur_l)
            if cur_l == 0:
                load_x(t)
                for c in conv0:
                    convert(*c)
        run_tile(t, sched[ti + 1] if ti + 1 < len(sched) else None)
    while conv_pending:
        convert(*conv_pending.pop(0))
    allb = b_xts[0] + b_xts[1] + [b_cst] + b_stg + b_stg2 + b_ar
    S.finish(allb, eng="sp")
    assert wstate["taken"] == len(wq) and wstate["released"] == len(wq)
    return nc


_PROG = None


def _host_consts():
    ident = np.eye(128, dtype=np.float32)
    jj = np.arange(128)
    trilT = (jj[:, None] <= jj[None, :]).astype(np.float32)
    bd = np.zeros((128, 128), np.float32)
    bd[:64, :64] = 1.0
    bd[64:, 64:] = 1.0
    return ident, trilT, bd


def kernel(x_prompt, x_sample, cache_attn_k, cache_attn_v, cache_conv, norm_mix_g, w_in, b_gate,
           gmlp_norm_g, gmlp_ws, gmlp_bs, conv_dw, conv_b, conv_norm_g, q_norm_g, k_norm_g, rel_bias,
           w_branch, w_out, norm_ffn_g, w_gate_up, w_down):
    global _PROG
    f = lambda a: np.ascontiguousarray(np.asarray(a, dtype=np.float32))
    x_prompt, x_sample = f(x_prompt), f(x_sample)
    cache_attn_k, cache_attn_v, cache_conv = f(cache_attn_k), f(cache_attn_v), f(cache_conv)
    rel_bias = f(rel_bias)
    ident, trilT, bd = _host_consts()
    pc = np.zeros((2, 128, NPC), np.float32)
    for l in range(2):
        pc[l, :, 0:8] = f(norm_mix_g)[l].reshape(8, 128).T
        pc[l, :, 8:16] = f(norm_ffn_g)[l].reshape(8, 128).T
        pc[l, :, 16:40] = f(b_gate)[l].reshape(24, 128).T
        pc[l, :, 40:44] = f(conv_b)[l].reshape(4, 128).T
        pc[l, :, 44:48] = f(conv_norm_g)[l].reshape(4, 128).T
        pc[l, :, 48:172] = f(conv_dw)[l].reshape(31, 4, 128).transpose(2, 1, 0).reshape(128, 124)
        pc[l, :, 172] = np.tile(f(q_norm_g)[l], 2)
        pc[l, :, 173] = np.tile(f(k_norm_g)[l], 2)
        pc[l, :, 174:182] = np.broadcast_to(rel_bias[l, :, 256][None, :], (128, 8))
    gbc = np.ascontiguousarray(np.broadcast_to(f(gmlp_norm_g)[:, None, :], (2, 128, W)))
    bsb = np.ascontiguousarray(np.broadcast_to(f(gmlp_bs)[:, None, :, :], (2, 128, 4, 128)))
    wsT = np.ascontiguousarray(f(gmlp_ws).transpose(0, 3, 1, 2))
    kk = np.arange(128)[:, None]
    qq = np.arange(128)[None, :]
    relbT = np.zeros((2, 128, 8, 2, 128), np.float32)
    for jj_, off in ((0, 128), (1, 0)):
        dist = np.clip(qq - kk + off, -128, 128) + 128
        g = rel_bias[:, :, dist]
        relbT[:, :, :, jj_, :] = g.transpose(0, 2, 1, 3)
    relbT[:, 64:128, :, 1, 0:64] = -1e30

    in_maps = []
    for c in range(8):
        b, seg = c // 4, c % 4
        s0 = seg * SEG
        xp = np.zeros((HALO + SEG, D), np.float32)
        lo = max(0, s0 - HALO)
        xp[HALO - (s0 - lo):] = x_prompt[b, lo:s0 + SEG]
        xs = np.zeros((128, D), np.float32)
        xs[:64] = x_sample[c]
        in_maps.append({
            "xp": xp, "xs": xs,
            "ck": np.ascontiguousarray(cache_attn_k[:, c].reshape(2, 512, W)),
            "cv": np.ascontiguousarray(cache_attn_v[:, c].reshape(2, 512, W)),
            "cc": np.ascontiguousarray(cache_conv[:, c]),
            "flag": np.full((128, 1), 0.0 if seg == 0 else 1.0, np.float32),
            "w_in": f(w_in), "w_branch": f(w_branch), "w_out": f(w_out), "w_gate_up": f(w_gate_up), "w_down": f(w_down),
            "pc": pc, "gbc": gbc, "bsb": bsb, "wsT": wsT, "relbT": relbT, "ident": ident, "trilT": trilT, "bd": bd,
        })
    if _PROG is None:
        _PROG = build_program()
    res = run_bass_kernel_spmd(_PROG, in_maps, core_ids=list(range(8)))
    r = res.results
    y_prompt = np.stack([np.concatenate([r[b * 4 + s]["y_p"] for s in range(4)], axis=0) for b in range(2)])
    y_sample = np.stack([r[c]["y_s"] for c in range(8)])
    kp = np.stack([r[b * 4 + 3]["kp"] for b in range(2)], axis=1).reshape(2, 2, 512, 8, 64)
    vp = np.stack([r[b * 4 + 3]["vp"] for b in range(2)], axis=1).reshape(2, 2, 512, 8, 64)
    cp = np.stack([r[b * 4 + 3]["cp"] for b in range(2)], axis=1)
    ks = np.stack([r[c]["ks"] for c in range(8)], axis=1).reshape(2, 8, 64, 8, 64)
    vs = np.stack([r[c]["vs"] for c in range(8)], axis=1).reshape(2, 8, 64, 8, 64)
    cs = np.stack([r[c]["cs"] for c in range(8)], axis=1)
    gs = np.stack([r[c]["gs"] for c in range(8)], axis=1)
    return (y_prompt.astype(np.float32), y_sample.astype(np.float32), kp, vp, cp, ks, vs, cs, gs)
```

```python
import numpy as np
import concourse.bass as bass
import concourse.mybir as mybir
from concourse.bass_utils import run_bass_kernel_spmd

F32 = mybir.dt.float32
F32R = mybir.dt.float32r
BF16 = mybir.dt.bfloat16
AF = mybir.ActivationFunctionType
ALU = mybir.AluOpType

D = 1024
W = 512
NCOL = 6656
FH = 2816
EPS = 1e-6
SEG = 4096
HALO = 1024
NPC = 182


class Buf:
    __slots__ = ("name", "w", "r", "dsem", "dcount")

    def __init__(self, name):
        self.name = name
        self.w = None
        self.r = {}
        self.dsem = None
        self.dcount = 0


class Sync:
    def __init__(self, nc):
        self.nc = nc
        self.engs = {"pe": nc.tensor, "dve": nc.vector, "act": nc.scalar, "pool": nc.gpsimd, "sp": nc.sync}
        self.sem = {k: nc.alloc_semaphore("s_" + k) for k in self.engs}
        self.cnt = {k: 0 for k in self.engs}
        self.waited = {k: {} for k in self.engs}

    def _need(self, eng, deps):
        e = self.engs[eng]
        best = {}
        for (so, sn, v) in deps:
            if eng == "pe" and sn == "s_pe":
                continue
            if v > best.get(sn, (None, 0))[1]:
                best[sn] = (so, v)
        for sn, (so, v) in best.items():
            if self.waited[eng].get(sn, 0) >= v:
                continue
            e.wait_ge(so, v)
            self.waited[eng][sn] = v

    def _deps(self, reads, writes):
        deps = []
        for b in reads:
            if b.w is not None:
                deps.append(b.w)
        for b in writes:
            if b.w is not None:
                deps.append(b.w)
            deps.extend(b.r.values())
        return deps

    def _mark(self, tag, reads, writes):
        for b in reads:
            b.r[tag[1]] = tag
        for b in writes:
            b.w = tag
            b.r = {}

    def op(self, eng, fn, reads=(), writes=()):
        self._need(eng, self._deps(reads, writes))
        ins = fn(self.engs[eng])
        self.cnt[eng] += 1
        ins.then_inc(self.sem[eng], 1)
        self._mark((self.sem[eng], "s_" + eng, self.cnt[eng]), reads, writes)
        return ins

    def mmg(self, fns, reads=(), writes=()):
        self._need("pe", self._deps(reads, writes))
        ins = None
        for fn in fns:
            ins = fn(self.engs["pe"])
        self.cnt["pe"] += 1
        ins.then_inc(self.sem["pe"], 1)
        self._mark((self.sem["pe"], "s_pe", self.cnt["pe"]), reads, writes)
        return ins

    def dma(self, q, out, in_, reads=(), writes=(), track=None):
        self._need(q, self._deps(reads, writes))
        t = track if track is not None else (writes[0] if writes else reads[0])
        if t.dsem is None:
            t.dsem = self.nc.alloc_semaphore("d_" + t.name)
        ins = self.engs[q].dma_start(out=out, in_=in_)
        ins.then_inc(t.dsem, 16)
        t.dcount += 16
        self._mark((t.dsem, "d_" + t.name, t.dcount), reads, writes)
        return ins

    def finish(self, bufs, eng="sp"):
        deps = []
        for b in bufs:
            if b.w is not None:
                deps.append(b.w)
            deps.extend(b.r.values())
        self._need(eng, deps)


def R(ap):
    return ap


def build_program():
    nc = bass.Bass("TRN2", target_bir_lowering=False)
    S = Sync(nc)

    def din(name, shape):
        return nc.dram_tensor(name, list(shape), F32, kind="ExternalInput").ap()

    def dout(name, shape):
        return nc.dram_tensor(name, list(shape), F32, kind="ExternalOutput").ap()

    xp = din("xp", [HALO + SEG, D])
    xs = din("xs", [128, D])
    ck = din("ck", [2, 512, W])
    cv = din("cv", [2, 512, W])
    cc = din("cc", [2, 30, W])
    flag = din("flag", [128, 1])
    w_in = din("w_in", [2, D, NCOL])
    w_branch = din("w_branch", [2, 3, W, D])
    w_out = din("w_out", [2, D, D])
    w_gu = din("w_gate_up", [2, D, 2 * FH])
    w_down = din("w_down", [2, FH, D])
    pc_d = din("pc", [2, 128, NPC])
    gbc_d = din("gbc", [2, 128, W])
    bsb_d = din("bsb", [2, 128, 4, 128])
    wsT_d = din("wsT", [2, 128, 4, 128])
    relbT_d = din("relbT", [2, 128, 8, 2, 128])
    ident_d = din("ident", [128, 128])
    tril_d = din("trilT", [128, 128])
    bd_d = din("bd", [128, 128])

    y_p = dout("y_p", [SEG, D])
    y_s = dout("y_s", [64, D])
    kp_o = dout("kp", [2, 512, W])
    vp_o = dout("vp", [2, 512, W])
    cp_o = dout("cp", [2, 30, W])
    ks_o = dout("ks", [2, 64, W])
    vs_o = dout("vs", [2, 64, W])
    cs_o = dout("cs", [2, 30, W])
    gs_o = dout("gs", [2, 64, W])
    x1 = nc.dram_tensor("x1_scr", [SEG + 512, D], F32, kind="Internal").ap()
    x1s = nc.dram_tensor("x1s_scr", [128, D], F32, kind="Internal").ap()
    NCH = 38
    wsc = [nc.dram_tensor(f"wsc{l}", [NCH, 128, 4096], BF16, kind="Internal").ap() for l in range(2)]
    _g0 = [Buf(f"wsc0_g{g}") for g in range(8)]
    _g1 = Buf("wsc1")
    b_wsc = [[_g0[min(c // 5, 7)] for c in range(NCH)], [_g1 for c in range(NCH)]]
    b_x1 = [Buf(f"x1blk{j}") for j in range(9)]
    b_x1s = Buf("x1s")

    def sb(name, shape, dt=F32):
        return nc.alloc_sbuf_tensor("sb_" + name, list(shape), dt)

    xts = [sb(f"xt{j}", [128, 4, D]) for j in range(2)]
    b_xts = [[Buf(f"xt{j}_{i}") for i in range(4)] for j in range(2)]
    hT = sb("hT", [128, 8, 512], BF16); b_hT = Buf("hT")
    arena = sb("arena", [128, 22, 512], BF16)
    xnA = sb("xnA", [128, D]); xnB = sb("xnB", [128, D])
    stg = sb("stg", [128, 4, 512]); b_stg = [Buf(f"stg{i}") for i in range(4)]
    stg2 = sb("stg2", [128, 4, 512]); b_stg2 = [Buf(f"stg2_{i}") for i in range(4)]
    tB = [sb(f"tB{i}", [128, 512]) for i in range(2)]; b_tB = [Buf(f"tB{i}") for i in range(2)]
    b_ar = [Buf(f"ar{i}") for i in range(22)]
    gst = sb("gst", [128, 4, 32]); b_gst = Buf("gst")
    gluTb = sb("gluTb", [128, 4, 30 + 512], BF16); b_glub = Buf("gluTb")
    Dg = [sb("Dg0", [128, 16, 128], BF16)] * 2; b_Dg = [Buf("Dg0")] * 2
    identb = sb("identb", [128, 128], BF16); b_idb = Buf("identb")
    pcb = sb("pcb", [128, 124], BF16); b_pcb = Buf("pcb")
    bgh = sb("bgh", [128, 24]); b_bgh = Buf("bgh")
    qT = sb("qT", [128, 4, 512], BF16); b_qT = Buf("qT")
    kT = sb("kT", [128, 4, 1024], BF16); b_k = [Buf(f"kT{i}") for i in range(8)]
    Va = sb("Va", [128, 8, 4, 4, 64], BF16); b_v = [Buf(f"Va{i}") for i in range(8)]
    NWB = 5
    wb = [sb(f"wb{i}", [128, 8, 512], BF16) for i in range(NWB)]
    b_wb = [Buf(f"wb{i}") for i in range(NWB)]
    B8 = sb("B8", [128, 8, 2, 128], BF16); b_B8 = Buf("B8")
    PT = [sb(f"PT{i}", [128, 640], BF16) for i in range(4)]; b_PT = [Buf(f"PT{i}") for i in range(4)]
    tA = [sb(f"tA{i}", [128, 512]) for i in range(2)]; b_tA = [Buf(f"tA{i}") for i in range(2)]
    tR = [sb(f"tR{i}", [128, 512], BF16) for i in range(2)]; b_tR = [Buf(f"tR{i}") for i in range(2)]
    rec = sb("rec", [128, 128]); b_rec = Buf("rec")
    pc = sb("pc", [128, NPC]); b_pc = Buf("pc")
    gbc = sb("gbc", [128, W]); b_gbc = Buf("gbc")
    bsb = sb("bsb", [128, 4, 128]); b_bsb = Buf("bsb")
    WmT = sb("WmT", [128, 4, 128], BF16); b_WmT = Buf("WmT")
    ident = sb("ident", [128, 128]); b_id = Buf("ident")
    trilT = sb("trilT", [128, 128]); b_tril = Buf("trilT")
    bd = sb("bd", [128, 128], BF16); b_bd = Buf("bd")
    ones = sb("ones", [128, 128], BF16); b_ones = Buf("ones")
    onesb = sb("onesb", [128, 4, 2, 64], BF16); b_onesb = Buf("onesb")
    flagb = sb("flagb", [128, 4, 2, 64], BF16); b_flagb = Buf("flagb")
    flg = sb("flg", [128, 1]); b_flg = Buf("flg")
    epsc = sb("epsc", [128, 1]); b_epsc = Buf("epsc")
    sm = sb("sm", [128, 32]); b_sm = Buf("sm")
    cst = sb("cst", [32, W]); b_cst = Buf("cst")

    PA = [nc.alloc_psum_tensor(f"PA{i}", [128, 512], F32) for i in range(2)]; b_PA = [Buf(f"PA{i}") for i in range(2)]
    PB = [nc.alloc_psum_tensor(f"PB{i}", [128, 512], F32) for i in range(2)]; b_PB = [Buf(f"PB{i}") for i in range(2)]
    PC = [nc.alloc_psum_tensor(f"PC{i}", [128, 512], F32) for i in range(4)]; b_PC = [Buf(f"PC{i}") for i in range(4)]
    rot = {"A": 0, "B": 0, "C": 0, "PT": 0, "tA": 0, "tB": 0, "tR": 0}

    def nxt(key, n):
        i = rot[key]
        rot[key] = (i + 1) % n
        return i

    def pa():
        i = nxt("A", 2); return PA[i], b_PA[i]

    def pb():
        i = nxt("B", 2); return PB[i], b_PB[i]

    def pcn():
        i = nxt("C", 4); return PC[i], b_PC[i]

    def tmpA():
        i = nxt("tA", 2); return tA[i], b_tA[i]

    def tmpR():
        i = nxt("tR", 2); return tR[i], b_tR[i]

    def tmpB():
        i = nxt("tB", 2); return tB[i], b_tB[i]

    wq = []
    wstate = {"issued": 0, "taken": 0, "released": 0}

    def w_issue(upto):
        while wstate["issued"] < min(upto, len(wq)):
            n = wstate["issued"]
            name, l_, ci = wq[n]
            bi = n % NWB
            if l_ == 1:
                while conv_pending:
                    convert(*conv_pending.pop(0))
            S.dma("sp", wb[bi][:], wsc[l_][ci].rearrange("p (k c) -> p k c", k=8), reads=[b_wsc[l_][ci]],
                  writes=[b_wb[bi]], track=b_wb[bi])
            wstate["issued"] += 1

    conv_pending = []
    conv_tags = []
    CONV_INFLIGHT = 3

    def convert(l_, ci, parts):
        dst = wsc[l_][ci].rearrange("p (k c) -> p k c", k=8)
        for (ap, kc, c0, cols) in parts:
            if len(conv_tags) >= CONV_INFLIGHT:
                S._need("pool", [conv_tags[-CONV_INFLIGHT]])
            S.dma("pool", dst[:, 0:kc, c0:c0 + cols], ap.rearrange("(k p) c -> p k c", p=128),
                  writes=[b_wsc[l_][ci]], track=b_wsc[l_][ci])
            conv_tags.append(b_wsc[l_][ci].w)

    def take(name):
        n = wstate["taken"]
        assert wq[n][0] == name, (wq[n][0], name)
        assert n < wstate["released"] + NWB
        w_issue(wstate["released"] + NWB)
        wstate["taken"] += 1
        bi = n % NWB
        return R(wb[bi][:]), b_wb[bi]

    def rel(k=1):
        wstate["released"] += k
        assert wstate["released"] <= wstate["taken"]
        w_issue(wstate["released"] + NWB)
        if conv_pending and wstate["released"] % 2 == 0:
            convert(*conv_pending.pop(0))

    def chunk_list(l, kind):
        out = []
        wi = w_in[l]
        names = ["u", "v", "ga", "gb", "q", "k", "va"]

        def col(nm, c0=0, n=512):
            i = names.index(nm)
            return wi[:, i * 512 + c0:i * 512 + c0 + n]
        for j in range(2):
            out.append((f"gab{j}", [(col("ga", j * 256, 256), 8, 0, 256), (col("gb", j * 256, 256), 8, 256, 256)]))
        use = ["u", "v", "q", "k", "va"] if kind == "full" else ["k", "va"]
        for nm in use:
            out.append((nm, [(col(nm), 8, 0, 512)]))
        if kind != "full":
            return out
        for n in range(3):
            for half in range(2):
                for q in range(2):
                    c0 = 3584 + n * 1024 + half * 512 + q * 256
                    d0 = half * 512 + q * 256
                    out.append((f"gw{n}{half}{q}", [(wi[:, c0:c0 + 256], 8, 0, 256),
                                                    (w_branch[l, n][:, d0:d0 + 256], 4, 256, 256)]))
        for half in range(2):
            out.append((f"wo{half}", [(w_out[l][:, half * 512:(half + 1) * 512], 8, 0, 512)]))
        for i in range(11):
            out.append((f"GU{i}", [(w_gu[l][:, i * 256:(i + 1) * 256], 8, 0, 256),
                                   (w_gu[l][:, FH + i * 256:FH + (i + 1) * 256], 8, 256, 256)]))
        for half in range(2):
            for j in range(3):
                r0 = j * 1024
                r1 = min(r0 + 1024, FH)
                out.append((f"D{half}{j}", [(w_down[l][r0:r1, half * 512:(half + 1) * 512], (r1 - r0) // 128, 0, 512)]))
        return out

    sched = []
    for l in range(2):
        n = 0
        first = 0 if l == 0 else 1
        for i in range(first, 10):
            kind = "h" if i == first else "full"
            sb_, db_ = [], []
            if l == 0:
                src = xp[i * 512:(i + 1) * 512, :]
                dst = x1[(i - 1) * 512:i * 512, :] if i >= 1 else None
                if i >= 1:
                    db_ = [b_x1[i - 1]]
            else:
                src = x1[(i - 1) * 512:i * 512, :]
                sb_ = [b_x1[i - 1]]
                dst = y_p[(i - 2) * 512:(i - 1) * 512, :] if i >= 2 else None
            sched.append(dict(l=l, kind=kind, NS=4, src=src, dst=dst, base=(n % 2) * 4, last=(i == 9), sample=False,
                              postflag=(l == 0 and i == 1), srcb=sb_, dstb=db_))
            n += 1
        sched.append(dict(l=l, kind="full", NS=1, src=(xs if l == 0 else x1s), dst=(x1s if l == 0 else y_s),
                          base=4, last=False, sample=True, postflag=False,
                          srcb=([] if l == 0 else [b_x1s]), dstb=([b_x1s] if l == 0 else [])))
    chunk_idx = [{}, {}]
    conv0 = []
    for l in range(2):
        full_list = chunk_list(l, "full")
        assert len(full_list) == NCH
        for ci, (nm, parts) in enumerate(full_list):
            chunk_idx[l][nm] = ci
            if l == 0:
                conv0.append((0, ci, parts))
            else:
                conv_pending.append((1, ci, parts))
    for ti, t in enumerate(sched):
        t["xb"] = ti % 2
        for (nm, parts) in chunk_list(t["l"], t["kind"]):
            wq.append((nm, t["l"], chunk_idx[t["l"]][nm]))

    S.dma("pool", ident[:], ident_d, writes=[b_id])
    S.dma("pool", trilT[:], tril_d, writes=[b_tril])
    S.dma("pool", flg[:], flag, writes=[b_flg])
    S.dma("pool", tA[1][:, 0:128], bd_d, writes=[b_tA[1]])
    S.op("dve", lambda e: e.tensor_copy(bd[:], tA[1][:, 0:128]), reads=[b_tA[1]], writes=[b_bd])
    S.op("dve", lambda e: e.memset(ones[:], 1.0), writes=[b_ones])
    S.op("dve", lambda e: e.memset(onesb[:], 1.0), writes=[b_onesb])
    S.op("dve", lambda e: e.memset(epsc[:], EPS), writes=[b_epsc])
    S.op("dve", lambda e: e.tensor_scalar(out=flagb[:].rearrange("p a b c -> p (a b c)"),
                                          in0=onesb[:].rearrange("p a b c -> p (a b c)"),
                                          scalar1=flg[:, 0:1], scalar2=None, op0=ALU.mult),
         reads=[b_onesb, b_flg], writes=[b_flagb])
    S.op("dve", lambda e: e.memset(gluTb[:], 0.0), writes=[b_glub])
    S.op("dve", lambda e: e.tensor_copy(identb[:], ident[:]), reads=[b_id], writes=[b_idb])
    S.op("dve", lambda e: e.memset(kT[:], 0.0), writes=b_k)
    S.op("dve", lambda e: e.memset(Va[:].rearrange("p a b c d -> p (a b c d)"), 0.0), writes=b_v)

    C_GMIX, C_GFFN, C_BG, C_CB, C_CG, C_DW, C_GQ, C_GK, C_B256 = 0, 8, 16, 40, 44, 48, 172, 173, 174
    SM_NB, SM_GKF, SM_SS, SM_RS = 0, 10, 12, 16
    SM_SS2, SM_RS2 = 20, 24
    b_sm2 = Buf("sm2")

    def layer_setup(l):
        S.dma("pool", pc[:], pc_d[l], writes=[b_pc])
        S.dma("pool", gbc[:], gbc_d[l], writes=[b_gbc])
        S.dma("pool", bsb[:], bsb_d[l], writes=[b_bsb])
        S.dma("pool", tA[0][:], wsT_d[l].rearrange("p a b -> p (a b)"), writes=[b_tA[0]])
        S.dma("pool", stg[:].rearrange("p a b -> p (a b)"), relbT_d[l].rearrange("p h j q -> p (h j q)"),
              writes=b_stg, track=b_stg[0])
        for g in range(4):
            S.op("dve", lambda e, g=g: e.tensor_tensor(out=R(WmT[:, g, :]), in0=tA[0][:, g * 128:(g + 1) * 128], in1=trilT[:], op=ALU.mult),
                 reads=[b_tA[0], b_tril], writes=[b_WmT])
        S.op("dve", lambda e: e.tensor_copy(pcb[:], pc[:, C_DW:C_DW + 124]), reads=[b_pc], writes=[b_pcb])
        S.op("dve", lambda e: e.tensor_scalar(out=bgh[:], in0=pc[:, C_BG:C_BG + 24], scalar1=0.5, scalar2=None, op0=ALU.mult),
             reads=[b_pc], writes=[b_bgh])
        S.op("dve", lambda e: e.tensor_scalar(out=sm[:, SM_NB:SM_NB + 8], in0=pc[:, C_B256:C_B256 + 8], scalar1=-1.0,
                                              scalar2=None, op0=ALU.mult), reads=[b_pc], writes=[b_sm])
        S.op("dve", lambda e: e.tensor_scalar(out=sm[:, SM_GKF:SM_GKF + 1], in0=pc[:, C_GK:C_GK + 1], scalar1=flg[:, 0:1],
                                              scalar2=None, op0=ALU.mult), reads=[b_pc, b_flg], writes=[b_sm])
        stgf = stg[:].rearrange("p a b -> p (a b)")
        for h in range(8):
            S.op("dve", lambda e, h=h: e.tensor_scalar(out=B8[:, h].rearrange("p a b -> p (a b)"),
                                                       in0=stgf[:, h * 256:(h + 1) * 256],
                                                       scalar1=sm[:, SM_NB + h:SM_NB + h + 1], scalar2=8.0,
                                                       op0=ALU.add, op1=ALU.mult),
                 reads=b_stg + [b_sm], writes=[b_B8])

    xnb = [(xnA[:], [Buf("xnA")]), (xnB[:], [Buf("xnB")])]

    def norm_to_hT(NS, goff, xt, b_xt):
        for c in norm_closures(NS, goff, xt, b_xt):
            c()

    def norm_closures(NS, goff, xt, b_xt):
        def stats():
            for s in range(NS):
                xn_, bxn_ = xnb[s % 2]
                S.op("act", lambda e: e.activation(R(xn_), xt[:, s, :], AF.Square, accum_out=sm[:, SM_SS2 + s:SM_SS2 + s + 1]),
                     reads=[b_xt[s]], writes=bxn_ + [b_sm2])
            S.op("act", lambda e: e.activation(sm[:, SM_RS2:SM_RS2 + NS], sm[:, SM_SS2:SM_SS2 + NS], AF.Sqrt,
                                               scale=1.0 / D, bias=EPS), reads=[b_sm2], writes=[b_sm2])
            S.op("dve", lambda e: e.reciprocal(sm[:, SM_RS2:SM_RS2 + NS], sm[:, SM_RS2:SM_RS2 + NS]), reads=[b_sm2], writes=[b_sm2])
        return [stats] + [(lambda s=s: norm_part(s, goff, xt, b_xt)) for s in range(NS)]

    def norm_part(s, goff, xt, b_xt):
        if True:
            xn_, bxn_ = xnb[s % 2]
            S.op("act", lambda e: e.activation(R(xn_), xt[:, s, :], AF.Copy, scale=sm[:, SM_RS2 + s:SM_RS2 + s + 1]),
                 reads=[b_xt[s], b_sm2], writes=bxn_)
            for kk in range(2):
                p_, bp_ = pa() if kk == 0 else pb()
                S.mmg([(lambda e, j=j: e.transpose(p_[:, j * 128:(j + 1) * 128],
                                                   xn_[:, (kk * 4 + j) * 128:(kk * 4 + j + 1) * 128], ident[:]))
                       for j in range(4)], reads=bxn_ + [b_id], writes=[bp_])
                S.op("dve", lambda e: e.tensor_tensor(
                    out=R(hT[:, kk * 4:kk * 4 + 4, s * 128:(s + 1) * 128]),
                    in0=p_[:, :].rearrange("p (j t) -> p j t", j=4),
                    in1=pc[:, goff + kk * 4:goff + kk * 4 + 4].unsqueeze(2).to_broadcast([128, 4, 128]), op=ALU.mult),
                     reads=[bp_, b_pc], writes=[b_hT])

    def proj_fm(dst, bdst, Wt, bW, cg, T):
        S.mmg([(lambda e, k=k: e.matmul(dst[:, 0:T], lhsT=Wt[:, k, cg * 128:(cg + 1) * 128], rhs=R(hT[:, k, 0:T]),
                                        start=(k == 0), stop=(k == 7))) for k in range(8)],
              reads=[bW, b_hT], writes=[bdst])

    def proj_tm(dst, bdst, Wt, bW, s):
        S.mmg([(lambda e, k=k: e.matmul(dst[:, 0:512], lhsT=R(hT[:, k, s * 128:(s + 1) * 128]), rhs=Wt[:, k, 0:512],
                                        start=(k == 0), stop=(k == 7))) for k in range(8)],
              reads=[bW, b_hT], writes=[bdst])

    def transpose_out(src_ap_fn, rows, ncols_tok, dst_tile, bsrc, bdst, dcol0=0):
        for cg in range(4):
            p_, bp_ = pa()
            S.mmg([lambda e: e.transpose(p_[0:ncols_tok, 0:128], src_ap_fn(cg), ident[:])], reads=[bsrc, b_id], writes=[bp_])
            S.op("dve", lambda e: e.tensor_copy(dst_tile[0:ncols_tok, cg * 128:(cg + 1) * 128], p_[0:ncols_tok, 0:128]),
                 reads=[bp_], writes=[bdst])

    def qk_norm(Wt, bW, T, gcol_ap, gcol_bufs, out_fn, out_bufs, f32_out=None, after=None):
        for cg in range(4):
            p_, bp_ = pa()
            proj_fm(p_, bp_, Wt, bW, cg, T)
            qs, bqs = tmpA()
            sr, bsr = tmpR()
            sq, bsq = tmpB()
            S.op("act", lambda e: e.activation(qs[:, 0:T], p_[:, 0:T], AF.Copy), reads=[bp_], writes=[bqs])
            S.op("act", lambda e: e.activation(R(sr[:, 0:T]), p_[:, 0:T], AF.Square), reads=[bp_], writes=[bsr])
            p2, bp2 = pb()
            S.mmg([lambda e: e.matmul(p2[:, 0:T], lhsT=R(bd[:]), rhs=R(sr[:, 0:T]), start=True, stop=True)],
                  reads=[bsr, b_bd], writes=[bp2])
            S.op("act", lambda e: e.activation(R(sq[:, 0:T]), p2[:, 0:T], AF.Ln, scale=1.0 / 64, bias=epsc[:, 0:1]),
                 reads=[bp2, b_epsc], writes=[bsq])
            S.op("act", lambda e: e.activation(R(sq[:, 0:T]), sq[:, 0:T], AF.Exp, scale=-0.5), reads=[bsq], writes=[bsq])
            S.op("dve", lambda e: e.scalar_tensor_tensor(out=out_fn(cg), in0=qs[:, 0:T], scalar=gcol_ap, in1=sq[:, 0:T],
                                                         op0=ALU.mult, op1=ALU.mult),
                 reads=[bqs, bsq] + gcol_bufs, writes=out_bufs)
            if f32_out is not None:
                f32_out(cg, qs, bqs, sq, bsq)
            if after is not None:
                after()

    def load_x(t):
        if t.get("loaded"):
            return
        t["loaded"] = True
        for s in range(t["NS"]):
            S.dma("sp", xts[t["xb"]][:, s, :], t["src"][s * 128:(s + 1) * 128, :], reads=t["srcb"],
                  writes=[b_xts[t["xb"]][s]], track=b_xts[t["xb"]][s])

    def run_tile(t, t_next):
        l, kind, NS, base = t["l"], t["kind"], t["NS"], t["base"]
        xt, b_xt = xts[t["xb"]], b_xts[t["xb"]]
        T = NS * 128
        full = kind == "full"
        sample = t["sample"]
        want_out = t["last"] or sample
        slots = [(base + s) % 8 for s in range(NS)]
        bg = []

        def drain(n=1):
            for _ in range(n):
                if bg:
                    bg.pop(0)()

        if sample:
            for s4 in range(4):
                S.dma("pool", stg[:, s4, :], ck[l, s4 * 128:(s4 + 1) * 128, :], writes=[b_stg[s4]])
                S.dma("pool", stg2[:, s4, :], cv[l, s4 * 128:(s4 + 1) * 128, :], writes=[b_stg2[s4]])
            S.dma("pool", cst[0:30, :], cc[l], writes=[b_cst])
            for s4 in range(4):
                for cg in range(4):
                    p_, bp_ = pa()
                    S.mmg([lambda e: e.transpose(p_[:, 0:128], stg[:, s4, cg * 128:(cg + 1) * 128], ident[:])],
                          reads=[b_stg[s4], b_id], writes=[bp_])
                    S.op("dve", lambda e: e.tensor_copy(kT[:, cg, s4 * 128:(s4 + 1) * 128], p_[:, 0:128]),
                         reads=[bp_], writes=[b_k[s4]])
                S.op("dve", lambda e: e.tensor_copy(Va[:, s4, :, 0:4:3, :],
                                                    stg2[:, s4, :].rearrange("p (a b c) -> p a b c", a=4, b=2)),
                     reads=[b_stg2[s4]], writes=[b_v[s4]])
                S.op("dve", lambda e: e.tensor_copy(Va[:, s4, :, 1:3, :], onesb[:]), reads=[b_onesb], writes=[b_v[s4]])
            for cg in range(4):
                p_, bp_ = pa()
                S.mmg([lambda e: e.transpose(p_[:, 0:30], cst[0:30, cg * 128:(cg + 1) * 128], ident[0:30, 0:30])],
                      reads=[b_cst, b_id], writes=[bp_])
                S.op("act", lambda e: e.activation(gluTb[:, cg, 0:30], p_[:, 0:30], AF.Copy), reads=[bp_], writes=[b_glub])
        elif full:
            S.op("dve", lambda e: e.tensor_copy(gluTb[:, :, 0:30], gluTb[:, :, 512:542]), reads=[b_glub], writes=[b_glub])
        load_x(t)
        if not t.get("normed"):
            norm_to_hT(NS, C_GMIX, xt, b_xt)

        fl = flg[:, 0:1]
        for cg in range(4):
            if cg % 2 == 0:
                if cg:
                    rel()
                Wa, bWa = take(f"gab{cg // 2}")
            p1, bp1 = pa()
            p2, bp2 = pb()
            proj_fm(p1, bp1, Wa, bWa, cg % 2, T)
            proj_fm(p2, bp2, Wa, bWa, 2 + cg % 2, T)
            ta, bta = tmpA()
            S.op("act", lambda e: e.activation(ta[:, 0:T], p2[:, 0:T], AF.Sigmoid), reads=[bp2], writes=[bta])
            if full:
                S.op("dve", lambda e: e.tensor_tensor(out=gluTb[:, cg, 30:30 + T], in0=p1[:, 0:T], in1=ta[:, 0:T], op=ALU.mult),
                     reads=[bp1, bta], writes=[b_glub])
            else:
                S.op("dve", lambda e: e.scalar_tensor_tensor(out=gluTb[:, cg, 30:30 + T], in0=p1[:, 0:T], scalar=fl,
                                                             in1=ta[:, 0:T], op0=ALU.mult, op1=ALU.mult),
                     reads=[bp1, bta, b_flg], writes=[b_glub])
            if want_out:
                o0 = (T - 30) if not sample else 34
                S.op("dve", lambda e: e.tensor_tensor(out=gst[:, cg, 0:30], in0=p1[:, o0:o0 + 30], in1=ta[:, o0:o0 + 30],
                                                      op=ALU.mult), reads=[bp1, bta], writes=[b_gst])
        rel()
        if want_out:
            transpose_out(lambda cg: gst[:, cg, 0:30], 128, 30, cst, b_gst, b_cst)
            S.dma("pool", (cs_o if sample else cp_o)[l], cst[0:30, :], reads=[b_cst])
        if full:
            for cg in range(4):
                for half in range(2):
                    def conv_part(cg=cg, half=half):
                        k0 = half * 16
                        nk = 16 if half == 0 else 15
                        d, bd_ = Dg[half], b_Dg[half]
                        c0 = cg * 31 + k0
                        S.op("dve", lambda e: e.tensor_tensor(out=d[:, 0:nk, :],
                                                              in0=identb[:].unsqueeze(1).to_broadcast([128, nk, 128]),
                                                              in1=pcb[:, c0:c0 + nk].unsqueeze(2).to_broadcast([128, nk, 128]),
                                                              op=ALU.mult),
                             reads=[b_idb, b_pcb], writes=[bd_])
                        S.mmg([(lambda e, i=i: e.matmul(PC[cg][:, 0:T], lhsT=d[:, i, :], rhs=gluTb[:, cg, k0 + i:k0 + i + T],
                                                        start=(k0 + i == 0), stop=(k0 + i == 30))) for i in range(nk)],
                              reads=[bd_, b_glub], writes=[b_PC[cg]])
                    bg.append(conv_part)

                def conv_evac(cg=cg):
                    cb = pc[:, C_CB + cg:C_CB + cg + 1]
                    S.op("dve", lambda e: e.tensor_scalar(out=R(arena[:, 8 + cg, 0:T]), in0=PC[cg][:, 0:T], scalar1=cb,
                                                          scalar2=None, op0=ALU.add),
                         reads=[b_PC[cg], b_pc], writes=[b_ar[8 + cg]])
                    S.op("act", lambda e: e.activation(R(arena[:, 12 + cg, 0:T]), PC[cg][:, 0:T], AF.Square, bias=cb, scale=1.0),
                         reads=[b_PC[cg], b_pc], writes=[b_ar[12 + cg]])
                bg.append(conv_evac)

        if full:
            Wt, bW = take("u")
            for cg in range(4):
                p_, bp_ = pa()
                proj_fm(p_, bp_, Wt, bW, cg, T)
                S.op("act", lambda e: e.activation(R(arena[:, cg, 0:T]), p_[:, 0:T], AF.Gelu), reads=[bp_], writes=[b_ar[cg]])
                drain()
            rel()
            Wt, bW = take("v")
            for s in range(NS):
                p_, bp_ = pa()
                proj_tm(p_, bp_, Wt, bW, s)
                vt = stg2[:, s, :]
                S.op("act", lambda e: e.activation(vt, p_[:, :], AF.Gelu), reads=[bp_], writes=[b_stg2[s]])
                drain()
            rel()
            for s in range(NS):
                vt = stg2[:, s, :]
                ta, bta = tmpA()
                S.op("act", lambda e: e.activation(ta[:], vt, AF.Square, accum_out=sm[:, SM_SS + s:SM_SS + s + 1]),
                     reads=[b_stg2[s]], writes=[bta, b_sm])
            S.op("act", lambda e: e.activation(sm[:, SM_RS:SM_RS + NS], sm[:, SM_SS:SM_SS + NS], AF.Sqrt,
                                               scale=1.0 / W, bias=EPS), reads=[b_sm], writes=[b_sm])
            S.op("dve", lambda e: e.reciprocal(sm[:, SM_RS:SM_RS + NS], sm[:, SM_RS:SM_RS + NS]), reads=[b_sm], writes=[b_sm])
            for s in range(NS):
                vt = stg2[:, s, :]
                S.op("dve", lambda e: e.scalar_tensor_tensor(out=R(arena[:, 4 + s, :]), in0=vt, scalar=sm[:, SM_RS + s:SM_RS + s + 1],
                                                             in1=gbc[:], op0=ALU.mult, op1=ALU.mult),
                     reads=[b_stg2[s], b_sm, b_gbc], writes=[b_ar[4 + s]])
                if sample:
                    S.op("dve", lambda e: e.scalar_tensor_tensor(out=vt, in0=vt, scalar=sm[:, SM_RS + s:SM_RS + s + 1],
                                                                 in1=gbc[:], op0=ALU.mult, op1=ALU.mult),
                         reads=[b_stg2[s], b_sm, b_gbc], writes=[b_stg2[s]])
                    S.dma("pool", gs_o[l], stg2[0:64, s, :], reads=[b_stg2[s]])
            Wt, bW = take("q")
            qk_norm(Wt, bW, T, pc[:, C_GQ:C_GQ + 1], [b_pc], lambda cg: qT[:, cg, 0:T], [b_qT], after=drain)
            rel()
        Wt, bW = take("k")
        kbufs = [b_k[sl] for sl in slots]

        def k_f32(cg, qs, bqs, sq, bsq):
            S.op("dve", lambda e: e.scalar_tensor_tensor(out=qs[:, 0:T], in0=qs[:, 0:T], scalar=pc[:, C_GK:C_GK + 1],
                                                         in1=sq[:, 0:T], op0=ALU.mult, op1=ALU.mult),
                 reads=[bqs, bsq, b_pc], writes=[bqs])
            for s in range(NS):
                p_, bp_ = pb()
                S.mmg([lambda e: e.transpose(p_[:, 0:128], qs[:, s * 128:(s + 1) * 128], ident[:])],
                      reads=[bqs, b_id], writes=[bp_])
                S.op("dve", lambda e: e.tensor_copy(stg[:, s, cg * 128:(cg + 1) * 128], p_[:, 0:128]),
                     reads=[bp_], writes=[b_stg[s]])

        gk_ap = pc[:, C_GK:C_GK + 1] if full else sm[:, SM_GKF:SM_GKF + 1]
        qk_norm(Wt, bW, T, gk_ap, [b_pc, b_sm], lambda cg: kT[:, cg, base * 128:base * 128 + T], kbufs,
                f32_out=(k_f32 if want_out else None), after=drain)
        rel()
        if want_out:
            for s in range(NS):
                if sample:
                    S.dma("pool", ks_o[l], stg[0:64, s, :], reads=[b_stg[s]])
                else:
                    S.dma("pool", kp_o[l, s * 128:(s + 1) * 128, :], stg[:, s, :], reads=[b_stg[s]])
        Wt, bW = take("va")
        for s in range(NS):
            sl = slots[s]
            p_, bp_ = pa()
            proj_tm(p_, bp_, Wt, bW, s)
            pv = p_[:, :].rearrange("p (a b c) -> p a b c", a=4, b=2)
            if full:
                S.op("act", lambda e: e.activation(Va[:, sl, :, 0:4:3, :], pv, AF.Copy), reads=[bp_], writes=[b_v[sl]])
                S.op("dve", lambda e: e.tensor_copy(Va[:, sl, :, 1:3, :], onesb[:]), reads=[b_onesb], writes=[b_v[sl]])
            else:
                S.op("act", lambda e: e.activation(Va[:, sl, :, 0:4:3, :], pv, AF.Copy, scale=fl), reads=[bp_, b_flg],
                     writes=[b_v[sl]])
                S.op("dve", lambda e: e.tensor_copy(Va[:, sl, :, 1:3, :], flagb[:]), reads=[b_flagb], writes=[b_v[sl]])
            if want_out:
                S.op("dve", lambda e: e.tensor_copy(stg2[:, s, :], p_[:, :]), reads=[bp_], writes=[b_stg2[s]])
                if sample:
                    S.dma("pool", vs_o[l], stg2[0:64, s, :], reads=[b_stg2[s]])
                else:
                    S.dma("pool", vp_o[l, s * 128:(s + 1) * 128, :], stg2[:, s, :], reads=[b_stg2[s]])
            drain()
        rel()
        if t_next is not None:
            load_x(t_next)
        if not full:
            return
        drain(len(bg))

        p2, bp2 = pb()
        S.mmg([(lambda e, cg=cg: e.matmul(p2[:, 0:T], lhsT=R(ones[:]), rhs=R(arena[:, 12 + cg, 0:T]), start=(cg == 0),
                                          stop=(cg == 3))) for cg in range(4)],
              reads=[b_ar[12 + cg] for cg in range(4)] + [b_ones], writes=[bp2])
        rs_, brs_ = tmpR()
        S.op("act", lambda e: e.activation(R(rs_[:, 0:T]), p2[:, 0:T], AF.Ln, scale=1.0 / W, bias=epsc[:, 0:1]),
             reads=[bp2, b_epsc], writes=[brs_])
        S.op("act", lambda e: e.activation(R(rs_[:, 0:T]), rs_[:, 0:T], AF.Exp, scale=-0.5), reads=[brs_], writes=[brs_])
        for cg in range(4):
            acc = arena[:, 8 + cg, 0:T]
            ta, bta = tmpA()
            S.op("dve", lambda e: e.tensor_tensor(out=ta[:, 0:T], in0=acc, in1=rs_[:, 0:T], op=ALU.mult),
                 reads=[b_ar[8 + cg], brs_], writes=[bta])
            S.op("act", lambda e: e.activation(R(acc), ta[:, 0:T], AF.Silu, scale=pc[:, C_CG + cg:C_CG + cg + 1]),
                 reads=[bta, b_pc], writes=[b_ar[8 + cg]])
        for g in range(4):
            p_, bp_ = pa()
            S.mmg([(lambda e, s=s: e.matmul(p_[:, s * 128:(s + 1) * 128], lhsT=R(arena[:, 4 + s, g * 128:(g + 1) * 128]),
                                            rhs=R(WmT[:, g, :]), start=True, stop=True)) for s in range(NS)],
                  reads=[b_ar[4 + s] for s in range(NS)] + [b_WmT], writes=[bp_])
            ta, bta = tmpA()
            for s in range(NS):
                S.op("dve", lambda e, s=s: e.tensor_tensor(out=ta[:, s * 128:(s + 1) * 128], in0=p_[:, s * 128:(s + 1) * 128],
                                                           in1=bsb[:, g, :], op=ALU.add), reads=[bp_, b_bsb], writes=[bta])
            S.op("dve", lambda e: e.tensor_tensor(out=R(arena[:, g, 0:T]), in0=arena[:, g, 0:T], in1=ta[:, 0:T], op=ALU.mult),
                 reads=[b_ar[g], bta], writes=[b_ar[g]])

        def attn_p1(u, s, p):
            wslots = [(base + s - 4 + j) % 8 for j in range(5)]
            X, bX = PC[2 + u % 2], b_PC[2 + u % 2]
            for e2 in range(2):
                h = 2 * p + e2
                lo, hi = 64 * e2, 64 * e2 + 64
                Sa, bSa = PC[e2], b_PC[e2]
                qs_ = qT[lo:hi, p, s * 128:(s + 1) * 128]
                fns = []
                for j in range(4):
                    fns.append(lambda e, j=j: e.matmul(Sa[:, j * 128:(j + 1) * 128],
                                                       lhsT=kT[lo:hi, p, wslots[j] * 128:(wslots[j] + 1) * 128],
                                                       rhs=qs_, start=True, stop=(j < 3)))
                fns.append(lambda e: e.matmul(Sa[:, 384:512], lhsT=identb[:], rhs=B8[:, h, 0, :], start=False, stop=True))
                S.mmg(fns, reads=[b_k[wslots[j]] for j in range(4)] + [b_qT, b_idb, b_B8], writes=[bSa])
                xs4 = X[:, 256 + e2 * 128:256 + (e2 + 1) * 128]
                S.mmg([lambda e: e.matmul(xs4, lhsT=kT[lo:hi, p, wslots[4] * 128:(wslots[4] + 1) * 128], rhs=qs_,
                                          start=True, stop=False),
                       lambda e: e.matmul(xs4, lhsT=identb[:], rhs=B8[:, h, 1, :], start=False, stop=True)],
                      reads=[b_k[wslots[4]], b_qT, b_idb, b_B8], writes=[bX])
                ip = (u % 2) * 2 + e2
                P_, bP_ = PT[ip], b_PT[ip]
                S.op("act", lambda e: e.activation(P_[:, 0:512], Sa[:, :], AF.Exp, scale=0.125), reads=[bSa], writes=[bP_])
                S.op("act", lambda e: e.activation(P_[:, 512:640], xs4, AF.Exp, scale=0.125), reads=[bX], writes=[bP_])
                S.op("dve", lambda e: e.memset(P_[0:64, 64:128], 0.0), writes=[bP_])

        def attn_p2(u, s, p):
            wslots = [(base + s - 4 + j) % 8 for j in range(5)]
            X, bX = PC[2 + u % 2], b_PC[2 + u % 2]
            for e2 in range(2):
                ip = (u % 2) * 2 + e2
                P_, bP_ = PT[ip], b_PT[ip]
                S.mmg([(lambda e, j=j: e.matmul(X[:, e2 * 128:(e2 + 1) * 128],
                                                lhsT=Va[:, wslots[j], p, 2 * e2:2 * e2 + 2, :].rearrange("p a b -> p (a b)"),
                                                rhs=P_[:, j * 128:(j + 1) * 128], start=(j == 0), stop=(j == 4)))
                       for j in range(5)],
                      reads=[b_v[wslots[j]] for j in range(5)] + [bP_], writes=[bX])
            S.op("dve", lambda e: e.reciprocal(rec[0:64, :], X[64:128, 0:128]), reads=[bX], writes=[b_rec])
            S.op("dve", lambda e: e.reciprocal(rec[64:128, :], X[0:64, 128:256]), reads=[bX], writes=[b_rec])
            S.op("dve", lambda e: e.tensor_tensor(out=R(arena[0:64, 12 + p, s * 128:(s + 1) * 128]), in0=X[0:64, 0:128],
                                                  in1=rec[0:64, :], op=ALU.mult), reads=[bX, b_rec], writes=[b_ar[12 + p]])
            S.op("dve", lambda e: e.tensor_tensor(out=R(arena[64:128, 12 + p, s * 128:(s + 1) * 128]), in0=X[64:128, 128:256],
                                                  in1=rec[64:128, :], op=ALU.mult), reads=[bX, b_rec], writes=[b_ar[12 + p]])

        units = [(s, p) for s in range(NS) for p in range(4)]
        for u in range(len(units) + 1):
            def step(u=u):
                if u < len(units):
                    attn_p1(u, *units[u])
                if u >= 1:
                    attn_p2(u - 1, *units[u - 1])
            bg.append(step)

        mslots = [4, 5, 6, 7, 16, 17, 18, 19]
        ysl = [0, 8, 12]
        for n in range(3):
            if n == 2:
                drain(len(bg))
            for half in range(2):
                for ee in range(4):
                    if ee % 2 == 0:
                        if ee:
                            rel()
                        Wg, bWg = take(f"gw{n}{half}{ee // 2}")
                        Wbr, bWbr = Wg, bWg
                    e_ = half * 4 + ee
                    ms = mslots[e_]
                    p2, bp2 = pb()
                    proj_fm(p2, bp2, Wg, bWg, ee % 2, T)
                    gt, bgt = tmpA()
                    S.op("act", lambda e: e.activation(gt[:, 0:T], p2[:, 0:T], AF.Tanh,
                                                       bias=bgh[:, n * 8 + e_:n * 8 + e_ + 1], scale=0.5),
                         reads=[bp2, b_bgh], writes=[bgt])
                    p1, bp1 = pa()
                    S.mmg([(lambda e, k=k: e.matmul(p1[:, 0:T], lhsT=Wbr[:, k, 256 + (ee % 2) * 128:256 + (ee % 2 + 1) * 128],
                                                    rhs=R(arena[:, ysl[n] + k, 0:T]), start=(k == 0), stop=(k == 3)))
                           for k in range(4)],
                          reads=[bWbr] + [b_ar[ysl[n] + k] for k in range(4)], writes=[bp1])
                    if n == 0:
                        S.op("dve", lambda e: e.scalar_tensor_tensor(out=R(arena[:, ms, 0:T]), in0=gt[:, 0:T], scalar=1.0,
                                                                     in1=p1[:, 0:T], op0=ALU.add, op1=ALU.mult),
                             reads=[bp1, bgt], writes=[b_ar[ms]])
                    else:
                        S.op("dve", lambda e: e.scalar_tensor_tensor(out=gt[:, 0:T], in0=gt[:, 0:T], scalar=1.0,
                                                                     in1=p1[:, 0:T], op0=ALU.add, op1=ALU.mult),
                             reads=[bp1, bgt], writes=[bgt])
                        S.op("dve", lambda e: e.tensor_tensor(out=R(arena[:, ms, 0:T]), in0=arena[:, ms, 0:T], in1=gt[:, 0:T],
                                                              op=ALU.add),
                             reads=[b_ar[ms], bgt], writes=[b_ar[ms]])
                    drain()
                rel()
        for half in range(2):
            Wt, bW = take(f"wo{half}")
            for s in range(NS):
                p1, bp1 = pa()
                S.mmg([(lambda e, k=k: e.matmul(p1[:, :], lhsT=R(arena[:, mslots[k], s * 128:(s + 1) * 128]), rhs=Wt[:, k, :],
                                                start=(k == 0), stop=(k == 7))) for k in range(8)],
                      reads=[bW] + [b_ar[m] for m in mslots], writes=[bp1])
                S.op("dve", lambda e: e.scalar_tensor_tensor(out=xt[:, s, half * 512:(half + 1) * 512], in0=p1[:, :], scalar=0.5,
                                                             in1=xt[:, s, half * 512:(half + 1) * 512], op0=ALU.mult, op1=ALU.add),
                     reads=[b_xt[s], bp1], writes=[b_xt[s]])
            rel()

        norm_to_hT(NS, C_GFFN, xt, b_xt)

        def ffn_cols(Wg, bWg, gcg, Wu, bWu, ucg, c):
            p1, bp1 = pa()
            p2, bp2 = pb()
            proj_fm(p1, bp1, Wg, bWg, gcg, T)
            proj_fm(p2, bp2, Wu, bWu, ucg, T)
            ta, bta = tmpA()
            S.op("act", lambda e: e.activation(ta[:, 0:T], p1[:, 0:T], AF.Silu), reads=[bp1], writes=[bta])
            S.op("dve", lambda e: e.tensor_tensor(out=R(arena[:, c, 0:T]), in0=p2[:, 0:T], in1=ta[:, 0:T], op=ALU.mult),
                 reads=[bp2, bta], writes=[b_ar[c]])

        for i in range(11):
            Wg, bWg = take(f"GU{i}")
            for cg in range(2):
                ffn_cols(Wg, bWg, cg, Wg, bWg, 2 + cg, 2 * i + cg)
            rel()
        if t_next is not None and t_next["l"] == l:
            t_next["normed"] = True
            bg.extend(norm_closures(t_next["NS"], C_GMIX, xts[t_next["xb"]], b_xts[t_next["xb"]]))
        for half in range(2):
            for j in range(3):
                Wt, bW = take(f"D{half}{j}")
                kc = 8 if j < 2 else 6
                for s in range(NS):
                    S.mmg([(lambda e, k=k: e.matmul(PC[s][:, :], lhsT=R(arena[:, 8 * j + k, s * 128:(s + 1) * 128]),
                                                    rhs=Wt[:, k, :], start=(j == 0 and k == 0), stop=(j == 2 and k == kc - 1)))
                           for k in range(kc)],
                          reads=[bW] + [b_ar[8 * j + k] for k in range(kc)], writes=[b_PC[s]])
                rel()
                drain()
            for s in range(NS):
                S.op("dve", lambda e: e.tensor_tensor(out=xt[:, s, half * 512:(half + 1) * 512],
                                                      in0=xt[:, s, half * 512:(half + 1) * 512], in1=PC[s][:, :], op=ALU.add),
                     reads=[b_xt[s], b_PC[s]], writes=[b_xt[s]])
        drain(len(bg))
        if t["postflag"]:
            for s in range(NS):
                S.op("dve", lambda e: e.tensor_copy(Va[:, slots[s], :, 1:3, :], flagb[:]), reads=[b_flagb],
                     writes=[b_v[slots[s]]])
        dst = t["dst"]
        for s in range(NS):
            if sample and l == 1:
                S.dma("pool", dst, xt[0:64, s, :], reads=[b_xt[s]], writes=t["dstb"], track=b_xt[s])
            else:
                S.dma("pool", dst[s * 128:(s + 1) * 128, :], xt[:, s, :], reads=[b_xt[s]], writes=t["dstb"], track=b_xt[s])

    cur_l = -1
    for ti, t in enumerate(sched):
        if t["l"] != cur_l:
            cur_l = t["l"]
            layer_setup(cur_l)
            if cur_l == 0:
                load_x(t)
                for c in conv0:
                    convert(*c)
        run_tile(t, sched[ti + 1] if ti + 1 < len(sched) else None)
    while conv_pending:
        convert(*conv_pending.pop(0))
    allb = b_xts[0] + b_xts[1] + [b_cst] + b_stg + b_stg2 + b_ar
    S.finish(allb, eng="sp")
    assert wstate["taken"] == len(wq) and wstate["released"] == len(wq)
    return nc


_PROG = None


def _host_consts():
    ident = np.eye(128, dtype=np.float32)
    jj = np.arange(128)
    trilT = (jj[:, None] <= jj[None, :]).astype(np.float32)
    bd = np.zeros((128, 128), np.float32)
    bd[:64, :64] = 1.0
    bd[64:, 64:] = 1.0
    return ident, trilT, bd


def kernel(x_prompt, x_sample, cache_attn_k, cache_attn_v, cache_conv, norm_mix_g, w_in, b_gate,
           gmlp_norm_g, gmlp_ws, gmlp_bs, conv_dw, conv_b, conv_norm_g, q_norm_g, k_norm_g, rel_bias,
           w_branch, w_out, norm_ffn_g, w_gate_up, w_down):
    global _PROG
    f = lambda a: np.ascontiguousarray(np.asarray(a, dtype=np.float32))
    x_prompt, x_sample = f(x_prompt), f(x_sample)
    cache_attn_k, cache_attn_v, cache_conv = f(cache_attn_k), f(cache_attn_v), f(cache_conv)
    rel_bias = f(rel_bias)
    ident, trilT, bd = _host_consts()
    pc = np.zeros((2, 128, NPC), np.float32)
    for l in range(2):
        pc[l, :, 0:8] = f(norm_mix_g)[l].reshape(8, 128).T
        pc[l, :, 8:16] = f(norm_ffn_g)[l].reshape(8, 128).T
        pc[l, :, 16:40] = f(b_gate)[l].reshape(24, 128).T
        pc[l, :, 40:44] = f(conv_b)[l].reshape(4, 128).T
        pc[l, :, 44:48] = f(conv_norm_g)[l].reshape(4, 128).T
        pc[l, :, 48:172] = f(conv_dw)[l].reshape(31, 4, 128).transpose(2, 1, 0).reshape(128, 124)
        pc[l, :, 172] = np.tile(f(q_norm_g)[l], 2)
        pc[l, :, 173] = np.tile(f(k_norm_g)[l], 2)
        pc[l, :, 174:182] = np.broadcast_to(rel_bias[l, :, 256][None, :], (128, 8))
    gbc = np.ascontiguousarray(np.broadcast_to(f(gmlp_norm_g)[:, None, :], (2, 128, W)))
    bsb = np.ascontiguousarray(np.broadcast_to(f(gmlp_bs)[:, None, :, :], (2, 128, 4, 128)))
    wsT = np.ascontiguousarray(f(gmlp_ws).transpose(0, 3, 1, 2))
    kk = np.arange(128)[:, None]
    qq = np.arange(128)[None, :]
    relbT = np.zeros((2, 128, 8, 2, 128), np.float32)
    for jj_, off in ((0, 128), (1, 0)):
        dist = np.clip(qq - kk + off, -128, 128) + 128
        g = rel_bias[:, :, dist]
        relbT[:, :, :, jj_, :] = g.transpose(0, 2, 1, 3)
    relbT[:, 64:128, :, 1, 0:64] = -1e30

    in_maps = []
    for c in range(8):
        b, seg = c // 4, c % 4
        s0 = seg * SEG
        xp = np.zeros((HALO + SEG, D), np.float32)
        lo = max(0, s0 - HALO)
        xp[HALO - (s0 - lo):] = x_prompt[b, lo:s0 + SEG]
        xs = np.zeros((128, D), np.float32)
        xs[:64] = x_sample[c]
        in_maps.append({
            "xp": xp, "xs": xs,
            "ck": np.ascontiguousarray(cache_attn_k[:, c].reshape(2, 512, W)),
            "cv": np.ascontiguousarray(cache_attn_v[:, c].reshape(2, 512, W)),
            "cc": np.ascontiguousarray(cache_conv[:, c]),
            "flag": np.full((128, 1), 0.0 if seg == 0 else 1.0, np.float32),
            "w_in": f(w_in), "w_branch": f(w_branch), "w_out": f(w_out), "w_gate_up": f(w_gate_up), "w_down": f(w_down),
            "pc": pc, "gbc": gbc, "bsb": bsb, "wsT": wsT, "relbT": relbT, "ident": ident, "trilT": trilT, "bd": bd,
        })
    if _PROG is None:
        _PROG = build_program()
    res = run_bass_kernel_spmd(_PROG, in_maps, core_ids=list(range(8)))
    r = res.results
    y_prompt = np.stack([np.concatenate([r[b * 4 + s]["y_p"] for s in range(4)], axis=0) for b in range(2)])
    y_sample = np.stack([r[c]["y_s"] for c in range(8)])
    kp = np.stack([r[b * 4 + 3]["kp"] for b in range(2)], axis=1).reshape(2, 2, 512, 8, 64)
    vp = np.stack([r[b * 4 + 3]["vp"] for b in range(2)], axis=1).reshape(2, 2, 512, 8, 64)
    cp = np.stack([r[b * 4 + 3]["cp"] for b in range(2)], axis=1)
    ks = np.stack([r[c]["ks"] for c in range(8)], axis=1).reshape(2, 8, 64, 8, 64)
    vs = np.stack([r[c]["vs"] for c in range(8)], axis=1).reshape(2, 8, 64, 8, 64)
    cs = np.stack([r[c]["cs"] for c in range(8)], axis=1)
    gs = np.stack([r[c]["gs"] for c in range(8)], axis=1)
    return (y_prompt.astype(np.float32), y_sample.astype(np.float32), kp, vp, cp, ks, vs, cs, gs)
```
